# Optimizing a Trainium2 kernel written in Bass

```python
import jax, jax.numpy as jnp
from jax import lax
import numpy as np

D_MODEL = 1024
BATCH = 8
SEQ = 4096
DEPTH = 2

EPS = 1e-6
N_BRANCH = 4
BRANCH_WIDTH = D_MODEL // 2
GRID_W = 64
FNET_GROUPS = 4
FNET_GROUP_DIM = BRANCH_WIDTH // FNET_GROUPS
NAT_HEADS = 8
NAT_HEAD_DIM = BRANCH_WIDTH // NAT_HEADS
WIN_ROWS = 8
WIN_COLS = 16
CONV_WIDTH = 31
CONV_PAD = CONV_WIDTH // 2
CHUNK = 128
SGU_GROUPS = 4
SGU_GROUP_DIM = BRANCH_WIDTH // SGU_GROUPS
W_ = BRANCH_WIDTH
IN_SPLITS = (W_, 4 * W_, 6 * W_, 8 * W_, 12 * W_)
IN_COLS = 12 * W_ + N_BRANCH * D_MODEL

kernel_name = "hybrid_gated_parallel_mixers_encoder"


def rms_norm(x, g):
    xf = x.astype(jnp.float32)
    y = xf * lax.rsqrt(jnp.mean(xf * xf, axis=-1, keepdims=True) + EPS)
    return y.astype(x.dtype) * g


def layer_norm(x, g, b):
    xf = x.astype(jnp.float32)
    mu = jnp.mean(xf, axis=-1, keepdims=True)
    var = jnp.mean(jnp.square(xf - mu), axis=-1, keepdims=True)
    y = (xf - mu) * lax.rsqrt(var + EPS)
    return y.astype(x.dtype) * g + b


def fourier_mixer(v):
    B, T, _ = v.shape
    vg = v.reshape(B, T, FNET_GROUPS, FNET_GROUP_DIM).astype(jnp.float32)
    y = jnp.fft.fft2(vg, axes=(1, 3), norm="ortho").real
    return y.reshape(B, T, BRANCH_WIDTH).astype(v.dtype)


def neighborhood_attention(q, k, v, rpb):
    B, T, H, Dh = q.shape
    R = T // GRID_W
    kh = min(WIN_ROWS, R)
    rows = jnp.arange(R)
    cols = jnp.arange(GRID_W)
    row_start = jnp.clip(rows - kh // 2, 0, R - kh)
    row_idx = row_start[:, None] + jnp.arange(kh)[None, :]
    col_start = jnp.clip(cols - WIN_COLS // 2, 0, GRID_W - WIN_COLS)
    in_win = (cols[None, :] >= col_start[:, None]) & (cols[None, :] < col_start[:, None] + WIN_COLS)
    qg = q.reshape(B, R, GRID_W, H, Dh)
    kg = k.reshape(B, R, GRID_W, H, Dh)[:, row_idx]
    vg = v.reshape(B, R, GRID_W, H, Dh)[:, row_idx]
    scores = jnp.einsum('brqhd,brikhd->bhrqik', qg, kg).astype(jnp.float32) * (Dh ** -0.5)
    dr = row_idx - rows[:, None] + (WIN_ROWS - 1)
    dc = jnp.clip(cols[None, :] - cols[:, None] + (WIN_COLS - 1), 0, 2 * WIN_COLS - 2)
    bias = rpb[:, dr[:, None, :, None], dc[None, :, None, :]]
    scores = scores + bias[None].astype(jnp.float32)
    scores = jnp.where(in_win[:, None, :], scores, -jnp.inf)
    probs = jax.nn.softmax(scores, axis=(-2, -1)).astype(v.dtype)
    out = jnp.einsum('bhrqik,brikhd->brqhd', probs, vg)
    return out.reshape(B, T, H * Dh)


def conv_module(a, conv_w, conv_b, ln_g, ln_b):
    ga, gb = jnp.split(a, 2, axis=-1)
    h = ga * jax.nn.sigmoid(gb)
    h = lax.conv_general_dilated(
        h, conv_w[:, None, :], window_strides=(1,), padding=[(CONV_PAD, CONV_PAD)],
        dimension_numbers=('NWC', 'WIO', 'NWC'), feature_group_count=BRANCH_WIDTH) + conv_b
    return jax.nn.silu(layer_norm(h, ln_g, ln_b))


def spatial_gating(uv, ln_g, ln_b, w_s, b_s):
    u, v = jnp.split(uv, 2, axis=-1)
    v = layer_norm(v, ln_g, ln_b)
    B, T, _ = v.shape
    vc = v.reshape(B, T // CHUNK, CHUNK, SGU_GROUPS, SGU_GROUP_DIM)
    mixed = jnp.einsum('gpq,bnqgc->bnpgc', w_s, vc) + b_s.T[:, :, None]
    return u * mixed.reshape(B, T, BRANCH_WIDTH)


def hybrid_layer(x, norm_g, w_in, rpb, conv_w, conv_b, conv_ln_g, conv_ln_b,
                 sgu_ln_g, sgu_ln_b, sgu_w, sgu_b, w_branch, w_out):
    B, T, _ = x.shape
    h = rms_norm(x, norm_g)
    z = h @ w_in
    a_val, qkv, c_in, d_in, br_gates, merge_logits = jnp.split(z, IN_SPLITS, axis=-1)
    q, k, v = jnp.split(qkv, 3, axis=-1)
    q = q.reshape(B, T, NAT_HEADS, NAT_HEAD_DIM)
    k = k.reshape(B, T, NAT_HEADS, NAT_HEAD_DIM)
    v = v.reshape(B, T, NAT_HEADS, NAT_HEAD_DIM)
    y_a = fourier_mixer(a_val)
    y_b = neighborhood_attention(q, k, v, rpb)
    y_c = conv_module(c_in, conv_w, conv_b, conv_ln_g, conv_ln_b)
    y_d = spatial_gating(d_in, sgu_ln_g, sgu_ln_b, sgu_w, sgu_b)
    ys = jnp.stack([y_a, y_b, y_c, y_d], axis=2)
    ys = ys * jax.nn.silu(br_gates.reshape(B, T, N_BRANCH, BRANCH_WIDTH))
    proj = jnp.einsum('btnw,nwd->btnd', ys, w_branch)
    gates = jax.nn.sigmoid(merge_logits.reshape(B, T, N_BRANCH, D_MODEL))
    merged = jnp.sum(gates * proj, axis=2)
    return x + merged @ w_out


def setup_inputs(seed: int = 0) -> dict:
    key = jax.random.key(seed)
    ks = jax.random.split(key, 16)
    f32 = jnp.float32
    L, D, W = DEPTH, D_MODEL, BRANCH_WIDTH
    return {
        "x": jax.random.normal(ks[0], (BATCH, SEQ, D), f32),
        "norm_g": 1.0 + 0.02 * jax.random.normal(ks[1], (L, D), f32),
        "w_in": jax.random.normal(ks[2], (L, D, IN_COLS), f32) * D ** -0.5,
        "nat_rpb": 0.02 * jax.random.normal(ks[3], (L, NAT_HEADS, 2 * WIN_ROWS - 1, 2 * WIN_COLS - 1), f32),
        "conv_w": jax.random.normal(ks[4], (L, CONV_WIDTH, W), f32) * CONV_WIDTH ** -0.5,
        "conv_b": 0.01 * jax.random.normal(ks[5], (L, W), f32),
        "conv_ln_g": 1.0 + 0.02 * jax.random.normal(ks[6], (L, W), f32),
        "conv_ln_b": 0.01 * jax.random.normal(ks[7], (L, W), f32),
        "sgu_ln_g": 1.0 + 0.02 * jax.random.normal(ks[8], (L, W), f32),
        "sgu_ln_b": 0.01 * jax.random.normal(ks[9], (L, W), f32),
        "sgu_w": jax.random.normal(ks[10], (L, SGU_GROUPS, CHUNK, CHUNK), f32) * CHUNK ** -0.5,
        "sgu_b": 1.0 + 0.02 * jax.random.normal(ks[11], (L, SGU_GROUPS, CHUNK), f32),
        "w_branch": jax.random.normal(ks[12], (L, N_BRANCH, W, D), f32) * W ** -0.5,
        "w_out": jax.random.normal(ks[13], (L, D, D), f32) * D ** -0.5,
        "final_g": 1.0 + 0.02 * jax.random.normal(ks[14], (D,), f32),
    }


def reference(x, norm_g, w_in, nat_rpb, conv_w, conv_b, conv_ln_g, conv_ln_b,
              sgu_ln_g, sgu_ln_b, sgu_w, sgu_b, w_branch, w_out, final_g):
    for l in range(DEPTH):
        x = hybrid_layer(x, norm_g[l], w_in[l], nat_rpb[l], conv_w[l], conv_b[l],
                         conv_ln_g[l], conv_ln_b[l], sgu_ln_g[l], sgu_ln_b[l],
                         sgu_w[l], sgu_b[l], w_branch[l], w_out[l])
    return rms_norm(x, final_g)
```

```python
import contextlib
import numpy as np
import ml_dtypes
import concourse.bass as bass
import concourse.mybir as mybir
from concourse.bass_utils import run_bass_kernel_spmd

F32 = mybir.dt.float32
BF16 = mybir.dt.bfloat16
AF = mybir.ActivationFunctionType
ALU = mybir.AluOpType
AX = mybir.AxisListType
NPBF = ml_dtypes.bfloat16

T = 4096
D = 1024
W = 512
L = 2
EPS = 1e-6
COMPUTE = ("pe", "act", "dve", "pool")


class Op:
    __slots__ = ("eng", "fn", "deps", "dma", "sem", "val", "need_inc", "prev_same_sem")


class Prog:
    def __init__(self):
        self.ops = []
        self.lastw = {}
        self.rd_eng = {}
        self.rd_dma = {}

    def add(self, eng, fn, r=(), w=(), dma=False, barrier=False):
        op = Op()
        op.eng, op.fn, op.dma = eng, fn, dma
        op.sem = None
        op.val = 0
        op.need_inc = False
        op.prev_same_sem = None
        r = tuple(r)
        w = tuple(w)
        if barrier:
            w = w + ("PHASE",)
        else:
            r = r + ("PHASE",)
        deps = []
        for k in r:
            lw = self.lastw.get(k)
            if lw is not None:
                deps.append(lw)
        for k in w:
            lw = self.lastw.get(k)
            if lw is not None:
                deps.append(lw)
            deps.extend(self.rd_eng.get(k, {}).values())
            deps.extend(self.rd_dma.get(k, []))
        for k in r:
            if dma:
                self.rd_dma.setdefault(k, []).append(op)
            else:
                self.rd_eng.setdefault(k, {})[eng] = op
        for k in w:
            self.lastw[k] = op
            self.rd_eng[k] = {}
            self.rd_dma[k] = []
        seen = set()
        op.deps = []
        for d in deps:
            if d is not op and id(d) not in seen:
                seen.add(id(d))
                op.deps.append(d)
        self.ops.append(op)
        return op

    def emit(self, nc, stack):
        ndma = {"sp": 8, "pool": 10, "act": 6}
        for op in self.ops:
            for d in op.deps:
                if d.dma:
                    continue
                if d.eng != op.eng or op.dma or d.eng != "pe":
                    d.need_inc = True
        sem_eng = {e: stack.enter_context(nc.semaphore("c_" + e)) for e in COMPUTE}
        cnt = {e: 0 for e in COMPUTE}
        dsem, dcnt, dlast, drr = {}, {}, {}, {}
        for q, n in ndma.items():
            dsem[q] = [stack.enter_context(nc.semaphore("d_%s%d" % (q, i))) for i in range(n)]
            dcnt[q] = [0] * n
            dlast[q] = [None] * n
            drr[q] = 0
        for op in self.ops:
            if op.dma:
                q = op.eng
                i = drr[q]
                drr[q] = (i + 1) % len(dsem[q])
                dcnt[q][i] += 16
                op.sem = dsem[q][i]
                op.val = dcnt[q][i]
                op.prev_same_sem = dlast[q][i]
                dlast[q][i] = op
            else:
                op.sem = sem_eng[op.eng]
                if op.need_inc:
                    cnt[op.eng] += 1
                op.val = cnt[op.eng]
        per = {}
        for op in self.ops:
            per.setdefault(op.eng, []).append(op)
        all_dma = [op for op in self.ops if op.dma]

        def run(eng_name, e):
            waited = {}

            def wait(sem, val):
                key = id(sem)
                if waited.get(key, 0) < val:
                    e.wait_ge(sem, val)
                    waited[key] = val

            for op in per.get(eng_name, []):
                for d in op.deps:
                    if (not d.dma) and d.eng == op.eng and (not op.dma) and d.eng == "pe":
                        continue
                    wait(d.sem, d.val)
                if op.dma and op.prev_same_sem is not None:
                    wait(op.prev_same_sem.sem, op.prev_same_sem.val)
                inst = op.fn(e)
                if op.dma:
                    inst.then_inc(op.sem, 16)
                elif op.need_inc:
                    inst.then_inc(op.sem, 1)
            if eng_name == "sp":
                fin = {}
                for op in all_dma:
                    fin[id(op.sem)] = (op.sem, max(fin.get(id(op.sem), (None, 0))[1], op.val))
                for sem, val in fin.values():
                    wait(sem, val)

        with nc.Block() as block:
            @block.tensor
            def _(e):
                run("pe", e)

            @block.scalar
            def _(e):
                run("act", e)

            @block.vector
            def _(e):
                run("dve", e)

            @block.gpsimd
            def _(e):
                run("pool", e)

            @block.sync
            def _(e):
                run("sp", e)


def host_consts():
    c = {}
    c["ident"] = np.eye(128, dtype=np.float32).astype(NPBF)
    m = np.arange(128)
    ang = 2 * np.pi * np.outer(m, m) / 128.0
    s128 = 1.0 / np.sqrt(128.0)
    c["cs128"] = np.concatenate([np.cos(ang) * s128, np.sin(ang) * s128], axis=1).astype(NPBF)
    t1 = np.arange(64)
    phi = 2 * np.pi * np.outer(t1, t1) / 64.0
    C, S = np.cos(phi) / 8.0, np.sin(phi) / 8.0
    f1 = np.zeros((128, 128), np.float64)
    f1[0:64, 0:64] = C
    f1[64:128, 0:64] = -S
    f1[0:64, 64:128] = S
    f1[64:128, 64:128] = C
    c["f1"] = f1.astype(NPBF)
    t2 = np.arange(64)[:, None, None]
    k1 = np.arange(64)[None, :, None]
    k2 = np.arange(64)[None, None, :]
    th = 2 * np.pi * t2 * (k1 + 64 * k2) / 4096.0
    dft = np.concatenate([np.cos(th) / 8.0, -np.sin(th) / 8.0], axis=0)
    c["dft"] = dft.reshape(128, 4096).astype(NPBF)
    obd = np.zeros((128, 128), np.float32)
    obd[0:64, 0:64] = 1
    obd[64:, 64:] = 1
    c["onesbd"] = obd.astype(NPBF)
    c["ones512"] = np.full((128, 128), 1.0 / 512.0, np.float32).astype(NPBF)
    c["neghalf"] = np.full((128, 1), -0.5, np.float32)
    return c


def attn_tables(rpb):
    kc = np.arange(64)[:, None]
    qc = np.arange(64)[None, :]
    cs = np.clip(qc - 8, 0, 48)
    inwin = (kc >= cs) & (kc < cs + 16)
    dc = np.clip(kc - qc + 15, 0, 30)
    g = rpb[:, :, ::-1, :][:, :, :, dc]
    g = np.where(inwin[None, None, None], g, np.float32(-30000.0))
    g = g.transpose(0, 1, 3, 2, 4)
    g = g.reshape(L, 4, 2, 64, 15, 64).reshape(L, 4, 128, 15 * 64)
    return np.ascontiguousarray(g.astype(np.float32))


def build(debug=None, nlayers=L, upto=99):
    nc = bass.Bass("TRN2", target_bir_lowering=False)
    P = Prog()

    def din(name, shape, dt=F32):
        return nc.dram_tensor(name, list(shape), dt, kind="ExternalInput")

    x_in = din("x", [T, D])
    ng_in = din("norm_g_bc", [L, 128, D])
    fg_in = din("final_g_bc", [128, D])
    win_in = din("w_in", [L, D, 10240])
    wbr_in = din("w_branch", [L, 4 * W, D])
    wout_in = din("w_out", [L, D, D])
    tab_in = din("attn_tab", [L, 4, 128, 960])
    convw_in = din("conv_wT", [L, W, 31])
    cvec_in = din("cvec", [L, W, 3])
    sgu_lng_in = din("sgu_ln_g_bc", [L, 128, W])
    sgu_lnb_in = din("sgu_ln_b_bc", [L, 128, W])
    sgu_wT_in = din("sgu_wT", [L, 128, 4, 128])
    sgu_b_in = din("sgu_b_bc", [L, 128, 4, 512])
    c_ident = din("c_ident", [128, 128], BF16)
    c_cs128 = din("c_cs128", [128, 256], BF16)
    c_f1 = din("c_f1", [128, 128], BF16)
    c_dft = din("c_dft", [128, 4096], BF16)
    c_onesbd = din("c_onesbd", [128, 128], BF16)
    c_ones512 = din("c_ones512", [128, 128], BF16)
    c_neghalf = din("c_neghalf", [128, 1])
    out_t = nc.dram_tensor("out", [T, D], F32, kind="ExternalOutput")

    def scratch(name, shape, dt=BF16):
        kind = "ExternalOutput" if (debug and (name in debug or (name == "yT" and "aT" in debug))) else "Internal"
        return nc.dram_tensor(name, list(shape), dt, kind=kind)

    qT_d = scratch("qT", [W, T])
    kT_d = scratch("kT", [W, T])
    vtok_d = scratch("vtok", [T, W])
    hc_d = scratch("hc", [W, T])
    uT_d = scratch("uT", [W, T])
    vn_d = scratch("vn", [T, W])
    gu_d = scratch("gu", [3 * W, T])
    thm_d = scratch("thm", [8, 128, 32, 512])
    ys_d = scratch("ys", [8, 128, 16, 512])
    x1_d = scratch("x1", [T, D], F32)
    yT_d = scratch("yT", [W, T])

    stack = contextlib.ExitStack()
    NBF = 103 * 1024
    SB = stack.enter_context(nc.sbuf_tensor("sb", [128, NBF], BF16))
    SBF = SB.bitcast(F32)
    PSH = [stack.enter_context(nc.psum_tensor("ps%d" % i, [128, 512], F32)) for i in range(8)]
    PS = [p[:] for p in PSH]
    PSB = [p.bitcast(BF16)[:] for p in PSH]

    class Arena:
        def __init__(self, base=0):
            self.off = base

        def bf(self, n):
            o = self.off
            self.off += (n + 31) // 32 * 32
            assert self.off <= NBF, self.off
            return SB[:, o:o + n]

        def f32(self, n):
            o = self.off
            self.off += (2 * n + 31) // 32 * 32
            assert self.off <= NBF, self.off
            return SBF[:, o // 2:o // 2 + n]

    def barrier():
        P.add("dve", lambda e: e.memset(bar_t, 0.0), barrier=True)

    A0 = Arena()
    ident = A0.bf(128)
    cs128 = A0.bf(256)
    f1 = A0.bf(128)
    onesbd = A0.bf(128)
    ones512 = A0.bf(128)
    neghalf = A0.f32(1)
    bar_t = A0.f32(8)
    eps_t = A0.f32(1)
    dft = A0.bf(4096)
    for ap, src in ((ident, c_ident), (cs128, c_cs128), (f1, c_f1), (onesbd, c_onesbd),
                    (ones512, c_ones512), (neghalf, c_neghalf), (dft, c_dft)):
        P.add("sp", lambda e, a=ap, s=src: e.dma_start(out=a, in_=s.ap()), w=["const"], dma=True)
    P.add("dve", lambda e: e.memset(eps_t, EPS), w=["const"])
    BASE = A0.off
    TOP0 = NBF - (16 * D + 8 * D + 32 * 512)
    TOPW = NBF - (16 * D + 8 * D)
    _AT = Arena(TOP0)
    thm0 = _AT.bf(32 * 512).rearrange("p (j t) -> p j t", j=32)
    wbr = _AT.bf(16 * D).rearrange("p (j d) -> p j d", j=16)
    wo = _AT.bf(8 * D).rearrange("p (j d) -> p j d", j=8)

    psrr = [0]

    psn = [4]

    def zbank():
        b = psrr[0] % psn[0]
        psrr[0] = (b + 1) % psn[0]
        return b

    for l in range(nlayers):
        xsrc = x_in if l == 0 else x1_d
        xdst = x1_d if l < L - 1 else out_t
        last = (l == L - 1)
        barrier()
        psn[0] = 4
        A1 = Arena(BASE)
        hT = A1.bf(8 * T).rearrange("p (k t) -> p k t", k=8)
        aT = A1.bf(4 * T).rearrange("p (g t) -> p g t", g=4)
        wbuf = [A1.bf(8 * 512).rearrange("p (k c) -> p k c", k=8) for _ in range(2)]
        g_bc = A1.f32(D)
        SH = A1.off
        xt = [A1.f32(D) for _ in range(8)]
        sqs = [A1.f32(D) for _ in range(4)]
        hb = [A1.bf(D) for _ in range(8)]
        ms = [A1.f32(1) for _ in range(8)]
        rstd = [A1.f32(1) for _ in range(8)]

        P.add("sp", lambda e, l=l: e.dma_start(out=g_bc, in_=ng_in.ap()[l]), w=["g_bc"], dma=True)
        wcnt = [0]

        def load_w(c0, ncols=512, l=l):
            b = wcnt[0] % 2
            wcnt[0] += 1
            src = win_in.ap()[l][:, c0:c0 + ncols].rearrange("(k p) c -> p k c", p=128)
            P.add("pool", lambda e, b=b, src=src: e.dma_start(out=wbuf[b][:, :, 0:ncols], in_=src),
                  w=[("wbuf", b)], dma=True)
            return b

        wA = load_w(0)
        xs = xsrc.ap()
        def zmm(e, bank, wb, cg, blk, ncols=128):
            for k in range(8):
                ins = e.matmul(PS[bank][0:ncols, :], lhsT=wbuf[wb][:, k, cg * 128:cg * 128 + ncols],
                               rhs=hT[:, k, blk * 512:(blk + 1) * 512], start=(k == 0), stop=(k == 7))
            return ins

        def zgroup(wb, cg, blk):
            bank = zbank()
            P.add("pe", lambda e: zmm(e, bank, wb, cg, blk),
                  r=[("wbuf", wb)] + [("hT", 4 * blk + j) for j in range(4)], w=[("ps", bank)])
            return bank

        def p1_stats(grp):
            tiles = list(range(grp * 4, grp * 4 + 4))
            for i in tiles:
                b = i % 8
                P.add("sp", lambda e, i=i, b=b, xs=xs: e.dma_start(out=xt[b], in_=xs[i * 128:(i + 1) * 128, :]),
                      r=[("x", l)], w=[("xt", b)], dma=True)
            for i in tiles:
                b = i % 8
                P.add("act", lambda e, b=b, sq=sqs[i % 4]: e.activation(out=sq, in_=xt[b], func=AF.Square, scale=1.0 / 32.0,
                                                                        accum_out=ms[b]),
                      r=[("xt", b)], w=[("sq", i % 4), ("ms", b)])
            for i in tiles:
                b = i % 8
                P.add("act", lambda e, b=b: e.activation(out=ms[b], in_=ms[b], func=AF.Ln, bias=eps_t),
                      r=[("ms", b), "const"], w=[("ms", b)])
            for i in tiles:
                b = i % 8
                P.add("act", lambda e, b=b: e.activation(out=rstd[b], in_=ms[b], func=AF.Exp, scale=-0.5),
                      r=[("ms", b)], w=[("rstd", b)])

        def p1_main(grp):
            tiles = list(range(grp * 4, grp * 4 + 4))
            for i in tiles:
                b = i % 8
                P.add("dve", lambda e, b=b: e.scalar_tensor_tensor(out=hb[b], in0=xt[b], scalar=rstd[b], in1=g_bc,
                                                                   op0=ALU.mult, op1=ALU.mult),
                      r=[("xt", b), ("rstd", b), "g_bc"], w=[("hb", b)])
            for i in tiles:
                b = i % 8
                pb = 4 + (i % 4)

                def tr(e, b=b, pb=pb):
                    for c in range(8):
                        ins = e.transpose(out=PSB[pb][:, c * 128:(c + 1) * 128], in_=hb[b][:, c * 128:(c + 1) * 128],
                                          identity=ident)
                    return ins
                P.add("pe", tr, r=[("hb", b), "const"], w=[("ps", pb)])

        def p1_evac(grp):
            tiles = list(range(grp * 4, grp * 4 + 4))
            for i in tiles:
                pb = 4 + (i % 4)
                if i % 2 == 0:
                    P.add("act", lambda e, i=i, pb=pb: e.activation(
                        out=hT[:, :, i * 128:(i + 1) * 128],
                        in_=PSB[pb][:, 0:1024].rearrange("p (k t) -> p k t", k=8), func=AF.Copy),
                        r=[("ps", pb)], w=[("hT", i)])
                else:
                    P.add("dve", lambda e, i=i, pb=pb: e.tensor_copy(
                        out=hT[:, :, i * 128:(i + 1) * 128],
                        in_=PSB[pb][:, 0:1024].rearrange("p (k t) -> p k t", k=8)),
                        r=[("ps", pb)], w=[("hT", i)])

        def p1_acols(blk):
            for g in range(4):
                bank = zgroup(wA, g, blk)
                if (g + blk) % 2 == 0:
                    P.add("act", lambda e, g=g, blk=blk, bank=bank: e.activation(
                        out=aT[:, g, blk * 512:(blk + 1) * 512], in_=PS[bank], func=AF.Copy),
                        r=[("ps", bank)], w=[("aT", g)])
                else:
                    P.add("dve", lambda e, g=g, blk=blk, bank=bank: e.tensor_copy(
                        out=aT[:, g, blk * 512:(blk + 1) * 512], in_=PS[bank]),
                        r=[("ps", bank)], w=[("aT", g)])

        p1_stats(0)
        for grp in range(8):
            if grp + 1 < 8:
                p1_stats(grp + 1)
            p1_main(grp)
            p1_evac(grp)
            if grp >= 1:
                p1_acols(grp - 1)
        p1_acols(7)
        if debug and "aT" in debug:
            for g in range(4):
                P.add("sp", lambda e, g=g: e.dma_start(out=yT_d.ap()[g * 128:(g + 1) * 128, :], in_=aT[:, g, :]),
                      r=[("aT", g)], w=[("yT_d", g)], dma=True)
        if upto <= 1:
            break
        wq_pref = load_w(512)
        barrier()
        psn[0] = 8
        A2 = Arena(SH)
        Qss = [A2.bf(64 * 128) for _ in range(2)]
        Ass = [A2.bf(128 * 64) for _ in range(2)]
        dftv = dft.rearrange("p (a b) -> p a b", b=64)
        evi = [0]

        def evac(out, in_, r, w):
            evi[0] += 1
            if evi[0] % 2 == 0:
                P.add("act", lambda e: e.activation(out=out, in_=in_, func=AF.Copy), r=r, w=w)
            else:
                P.add("dve", lambda e: e.tensor_copy(out=out, in_=in_), r=r, w=w)

        def fft_views(g):
            Qs, As = Qss[g % 2], Ass[g % 2]
            return (Qs, As, Qs.rearrange("p (t m) -> p t m", m=128), As.rearrange("p (m k) -> p m k", k=64),
                    ("Qs", g % 2), ("As", g % 2))

        def fft_s1(g):
            Qs, As, Qv, Av, kQ, kA = fft_views(g)
            for tb in range(16):
                bank = zbank()

                def s1(e, tb=tb, bank=bank):
                    for j in range(4):
                        t2 = tb * 4 + j
                        lh = aT[:, g, t2:T:64]
                        e.matmul(PS[bank][0:64, j * 128:(j + 1) * 128], lhsT=lh, rhs=cs128[:, 0:128],
                                 start=True, stop=True)
                        ins = e.matmul(PS[bank][64:128, j * 128:(j + 1) * 128], lhsT=lh, rhs=cs128[:, 128:256],
                                       start=True, stop=True)
                    return ins
                P.add("pe", s1, r=[("aT", g), "const"], w=[("ps", bank)])
                evac(Qs[:, tb * 512:(tb + 1) * 512], PS[bank], [("ps", bank)], [kQ])

        def fft_s2(g):
            Qs, As, Qv, Av, kQ, kA = fft_views(g)
            for mb in range(16):
                bank = zbank()

                def s2(e, mb=mb, bank=bank):
                    for j in range(8):
                        m = mb * 8 + j
                        lh = Qv[:, :, m]
                        e.matmul(PS[bank][0:64, j * 64:(j + 1) * 64], lhsT=lh, rhs=f1[:, 0:64],
                                 start=True, stop=True)
                        ins = e.matmul(PS[bank][64:128, j * 64:(j + 1) * 64], lhsT=lh, rhs=f1[:, 64:128],
                                       start=True, stop=True)
                    return ins
                P.add("pe", s2, r=[kQ, "const"], w=[("ps", bank)])
                evac(As[:, mb * 512:(mb + 1) * 512], PS[bank], [("ps", bank)], [kA])

        def fft_s3(g):
            Qs, As, Qv, Av, kQ, kA = fft_views(g)
            for kb in range(8):
                bank = zbank()

                def s3(e, kb=kb, bank=bank):
                    for j in range(8):
                        k1 = kb * 8 + j
                        ins = e.matmul(PS[bank][:, j * 64:(j + 1) * 64], lhsT=Av[:, :, k1], rhs=dftv[:, k1, :],
                                       start=True, stop=True)
                    return ins
                P.add("pe", s3, r=[kA, "const"], w=[("ps", bank)])
                evac(aT[:, g, kb * 512:(kb + 1) * 512], PS[bank], [("ps", bank)], [("aT", g)])

        fft_s1(0)
        fft_s2(0)
        fft_s1(1)
        fft_s3(0)
        fft_s2(1)
        fft_s1(2)
        fft_s3(1)
        fft_s2(2)
        fft_s1(3)
        fft_s3(2)
        fft_s2(3)
        fft_s3(3)
        if debug and "yT" in debug:
            for g in range(4):
                P.add("sp", lambda e, g=g: e.dma_start(out=yT_d.ap()[g * 128:(g + 1) * 128, :], in_=aT[:, g, :]),
                      r=[("aT", g)], w=[("yT_d", g)], dma=True)
        if upto <= 2:
            break

        barrier()
        A3 = Arena(SH)
        stage = [A3.bf(T) for _ in range(2)]
        tmpa = A3.bf(T)
        thb_ = [A3.bf(512) for _ in range(2)]
        vst = [A3.bf(512) for _ in range(4)]
        xcs = [A3.f32(512) for _ in range(4)]
        sqvs = [A3.f32(512) for _ in range(4)]
        means = [A3.f32(1) for _ in range(4)]
        vars_ = [A3.f32(1) for _ in range(4)]
        rs3s = [A3.f32(1) for _ in range(4)]
        lng = A3.f32(512)
        lnb = A3.f32(512)
        P.add("sp", lambda e, l=l: e.dma_start(out=lng, in_=sgu_lng_in.ap()[l]), w=["lng"], dma=True)
        P.add("sp", lambda e, l=l: e.dma_start(out=lnb, in_=sgu_lnb_in.ap()[l]), w=["lnb"], dma=True)
        stc = [0]
        thc = [0]

        def fm_chunk(wb, kind, dst, row0, wb2=None, only_g=None):
            for g in (range(4) if only_g is None else [only_g]):
                sb_ = stc[0] % 2
                stc[0] += 1
                if kind == "glu":
                    for blk in range(8):
                        bank = zgroup(wb, g, blk)
                        evac(tmpa[:, blk * 512:(blk + 1) * 512], PS[bank], [("ps", bank)], ["tmpa"])
                for blk in range(8):
                    sl = stage[sb_][:, blk * 512:(blk + 1) * 512]
                    bank = zgroup(wb2 if kind == "glu" else wb, g, blk)
                    if kind == "copy":
                        evac(sl, PS[bank], [("ps", bank)], [("stage", sb_)])
                    elif kind == "tanh":
                        P.add("act", lambda e, sl=sl, bank=bank: e.activation(out=sl, in_=PS[bank], func=AF.Sigmoid),
                              r=[("ps", bank)], w=[("stage", sb_)])
                    else:
                        tb = thc[0] % 2
                        thc[0] += 1
                        P.add("act", lambda e, tb=tb, bank=bank: e.activation(out=thb_[tb], in_=PS[bank], func=AF.Sigmoid),
                              r=[("ps", bank)], w=[("th", tb)])
                        if kind == "glu":
                            P.add("dve", lambda e, sl=sl, tb=tb, blk=blk: e.tensor_tensor(
                                out=sl, in0=thb_[tb], in1=tmpa[:, blk * 512:(blk + 1) * 512], op=ALU.mult),
                                r=[("th", tb), "tmpa"], w=[("stage", sb_)])
                        else:
                            P.add("dve", lambda e, sl=sl, tb=tb, bank=bank: e.tensor_tensor(
                                out=sl, in0=PS[bank], in1=thb_[tb], op=ALU.mult),
                                r=[("th", tb), ("ps", bank)], w=[("stage", sb_)])
                            if kind == "gateA":
                                yv = aT[:, g, :].rearrange("p (k1 k2) -> p k2 k1", k2=64)[:, blk * 8:(blk + 1) * 8, :]
                                slv = sl.rearrange("p (r c) -> p r c", c=64)
                                P.add("dve", lambda e, slv=slv, yv=yv: e.tensor_tensor(out=slv, in0=slv, in1=yv, op=ALU.mult),
                                      r=[("aT", g)], w=[("stage", sb_)])
                if dst is ys_d or dst is thm_d:
                    jj_ = (row0 + g * 128) // 128
                    P.add("sp", lambda e, sb_=sb_, jj_=jj_: e.dma_start(
                        out=dst.ap()[:, :, jj_, :].rearrange("b p t -> p b t"),
                        in_=stage[sb_].rearrange("p (b t) -> p b t", b=8)),
                        r=[("stage", sb_)], w=[(dst.name, row0 + g * 128)], dma=True)
                else:
                    P.add("sp", lambda e, sb_=sb_, g=g: e.dma_start(out=dst.ap()[row0 + g * 128:row0 + (g + 1) * 128, :], in_=stage[sb_]),
                          r=[("stage", sb_)], w=[(dst.name, row0 + g * 128)], dma=True)

        def tm_chunk(wb, dst, ln, only_grp=None):
            for i4_ in (range(8) if only_grp is None else [only_grp]):
                tiles = list(range(i4_ * 4, i4_ * 4 + 4))
                banks = {}
                for i in tiles:
                    bank = zbank()
                    banks[i] = bank

                    def mm(e, i=i, bank=bank):
                        for k in range(8):
                            ins = e.matmul(PS[bank], lhsT=hT[:, k, i * 128:(i + 1) * 128], rhs=wbuf[wb][:, k, :],
                                           start=(k == 0), stop=(k == 7))
                        return ins
                    P.add("pe", mm, r=[("wbuf", wb), ("hT", i)], w=[("ps", bank)])
                if not ln:
                    for i in tiles:
                        evac(vst[i % 4], PS[banks[i]], [("ps", banks[i])], [("vst", i % 4)])
                else:
                    for i in tiles:
                        v = i % 4
                        P.add("act", lambda e, bank=banks[i], sqv=sqvs[v], mean=means[v]: e.activation(
                            out=sqv, in_=PS[bank], func=AF.Identity, scale=1.0 / W, accum_out=mean),
                              r=[("ps", banks[i])], w=[("sqv", v), ("mean", v)])
                    for i in tiles:
                        v = i % 4
                        P.add("dve", lambda e, bank=banks[i], xc=xcs[v], mean=means[v]: e.tensor_scalar(
                            out=xc, in0=PS[bank], scalar1=mean, scalar2=None, op0=ALU.subtract),
                              r=[("ps", banks[i]), ("mean", v)], w=[("xc", v)])
                    for i in tiles:
                        v = i % 4
                        P.add("act", lambda e, xc=xcs[v], sqv=sqvs[v], var=vars_[v]: e.activation(
                            out=sqv, in_=xc, func=AF.Square, scale=float(W ** -0.5), accum_out=var),
                              r=[("xc", v)], w=[("sqv", v), ("var", v)])
                    for i in tiles:
                        v = i % 4
                        P.add("pool", lambda e, var=vars_[v]: e.tensor_scalar(out=var, in0=var, scalar1=EPS, scalar2=None, op0=ALU.add),
                              r=[("var", v)], w=[("var", v)])
                    for i in tiles:
                        v = i % 4
                        P.add("pool", lambda e, var=vars_[v], rs3=rs3s[v]: e.tensor_tensor(out=rs3, in0=var, in1=neghalf, op=ALU.pow),
                              r=[("var", v), "const"], w=[("rs3", v)])
                    for i in tiles:
                        v = i % 4
                        P.add("dve", lambda e, xc=xcs[v], rs3=rs3s[v]: e.scalar_tensor_tensor(
                            out=xc, in0=xc, scalar=rs3, in1=lng, op0=ALU.mult, op1=ALU.mult),
                              r=[("xc", v), ("rs3", v), "lng"], w=[("xc", v)])
                    for i in tiles:
                        v = i % 4
                        P.add("dve", lambda e, v=v, xc=xcs[v]: e.tensor_tensor(out=vst[v], in0=xc, in1=lnb, op=ALU.add),
                              r=[("xc", v), "lnb"], w=[("vst", v)])
                for i in tiles:
                    P.add("sp", lambda e, i=i: e.dma_start(out=dst.ap()[i * 128:(i + 1) * 128, :], in_=vst[i % 4]),
                          r=[("vst", i % 4)], w=[(dst.name, i)], dma=True)

        fm_chunk(wq_pref, "copy", qT_d, 0)
        fm_chunk(load_w(1024), "copy", kT_d, 0)
        tm_chunk(load_w(1536), vtok_d, False)
        wa_ = load_w(2048)
        wb_ = load_w(2560)
        fm_chunk(wa_, "glu", hc_d, 0, wb2=wb_)
        wu_ = load_w(3072)
        wv_ = load_w(3584)
        for kind_, idx_ in (("v", 0), ("u", 0), ("v", 1), ("v", 2), ("u", 1), ("v", 3), ("v", 4), ("u", 2),
                            ("v", 5), ("v", 6), ("u", 3), ("v", 7)):
            if kind_ == "u":
                fm_chunk(wu_, "copy", uT_d, 0, only_g=idx_)
            else:
                tm_chunk(wv_, vn_d, True, only_grp=idx_)
        fm_chunk(load_w(4096), "gateA", ys_d, 0)
        for n in range(3):
            fm_chunk(load_w(4608 + 512 * n), "gate", gu_d, 512 * n)
        for c in range(8):
            fm_chunk(load_w(6144 + 512 * c), "tanh", thm_d, 512 * c)
        if upto <= 3:
            break

        barrier()
        psn[0] = 4
        A4 = Arena(BASE)
        qhs = [A4.bf(T) for _ in range(2)]
        khs = [A4.bf(T) for _ in range(2)]
        guhs = [A4.bf(T) for _ in range(2)]
        ysts = [A4.bf(T) for _ in range(2)]
        Kbds = [A4.bf(64 * 128).rearrange("p (r c) -> p r c", c=128) for _ in range(2)]
        Vbds = [A4.bf(64 * 128).rearrange("p (r c) -> p r c", c=128) for _ in range(2)]
        tabs = [A4.f32(960) for _ in range(2)]
        tabb = [A4.bf(960) for _ in range(2)]
        Pt = [A4.bf(512) for _ in range(6)]
        recs = [A4.f32(512) for _ in range(2)]
        def zero_offdiag(s_):
            P.add("dve", lambda e: e.memset(Kbds[s_][0:64, :, 64:128], 0.0), w=[("KbdZ", s_, 0)])
            P.add("pool", lambda e: e.memset(Kbds[s_][64:128, :, 0:64], 0.0), w=[("KbdZ", s_, 1)])
            P.add("dve", lambda e: e.memset(Vbds[s_][0:64, :, 64:128], 0.0), w=[("VbdZ", s_, 0)])
            P.add("pool", lambda e: e.memset(Vbds[s_][64:128, :, 0:64], 0.0), w=[("VbdZ", s_, 1)])

        zero_offdiag(0)
        rs_ = [min(max(r - 4, 0), 56) for r in range(64)]
        wsf = A4.f32(512)
        wsT = A4.bf(512).rearrange("p (g q) -> p g q", g=4)
        sbb = A4.f32(2048).rearrange("p (g r) -> p g r", g=4)
        vng = A4.bf(32 * 128).rearrange("p (i c) -> p i c", i=32)
        ug6 = A4.bf(T)
        gug6 = A4.bf(T)
        st6 = A4.bf(T)
        mixs = [A4.bf(512) for _ in range(4)]
        assert A4.off <= NBF, A4.off
        P.add("sp", lambda e, l=l: e.dma_start(out=wsf, in_=sgu_wT_in.ap()[l].rearrange("q g p -> q (g p)")), w=["wsf"], dma=True)
        P.add("sp", lambda e, l=l: e.dma_start(out=sbb, in_=sgu_b_in.ap()[l]), w=["sbb"], dma=True)
        P.add("dve", lambda e: e.tensor_copy(out=wsT.rearrange("p g q -> p (g q)"), in_=wsf), r=["wsf"], w=["wsT"])

        def sgu_loads(g):
            P.add("sp", lambda e: e.dma_start(out=ug6, in_=uT_d.ap()[g * 128:(g + 1) * 128, :]),
                  r=[("uT", g * 128)], w=["ug6"], dma=True)
            P.add("sp", lambda e: e.dma_start(out=gug6, in_=gu_d.ap()[2 * W + g * 128:2 * W + (g + 1) * 128, :]),
                  r=[("gu", 2 * W + g * 128)], w=["gug6"], dma=True)
            for iq in range(4):
                P.add("sp", lambda e, iq=iq: e.dma_start(
                    out=vng[:, iq * 8:(iq + 1) * 8, :],
                    in_=vn_d.ap()[iq * 1024:(iq + 1) * 1024, g * 128:(g + 1) * 128].rearrange("(i p) c -> p i c", p=128)),
                    r=[("vn", i) for i in range(8 * iq, 8 * iq + 8)], w=[("vng", iq)], dma=True)
            for hq in range(2):
                hs = slice(hq * 2048, (hq + 1) * 2048)
                P.add("pool", lambda e, hs=hs: e.tensor_tensor(out=gug6[:, hs], in0=gug6[:, hs], in1=ug6[:, hs], op=ALU.mult),
                      r=["gug6", "ug6"], w=["gug6"])

        def sgu_bank(g, blk):
            bank = zbank()
            q6 = blk % 4
            bs = slice(blk * 512, (blk + 1) * 512)

            def smm(e):
                for c in range(4):
                    i = blk * 4 + c
                    ins = e.matmul(PS[bank][:, c * 128:(c + 1) * 128], lhsT=vng[:, i, :], rhs=wsT[:, g, :],
                                   start=True, stop=True)
                return ins
            P.add("pe", smm, r=[("vng", blk // 2), "wsT"], w=[("ps", bank)])
            P.add("dve", lambda e: e.tensor_tensor(out=mixs[q6], in0=PS[bank], in1=sbb[:, g, :], op=ALU.add),
                  r=[("ps", bank), "sbb"], w=[("mix", q6)])
            P.add("dve", lambda e: e.tensor_tensor(out=st6[:, bs], in0=mixs[q6], in1=gug6[:, bs], op=ALU.mult),
                  r=[("mix", q6), "gug6"], w=["st6"])
            if blk == 7:
                P.add("sp", lambda e: e.dma_start(out=ys_d.ap()[:, :, 12 + g, :].rearrange("b p t -> p b t"),
                                                  in_=st6.rearrange("p (b t) -> p b t", b=8)),
                      r=["st6"], w=[("ys", 3 * W + g * 128)], dma=True)

        def load_hp(hp, l=l, part="all"):
            s_ = hp % 2
            if part == "fill":
                P.add("dve", lambda e: e.tensor_scalar(out=tabb[s_], in0=tabs[s_], scalar1=8.0, scalar2=None, op0=ALU.mult),
                      r=[("tabf", s_)], w=[("tab", s_)])
                for half in range(2):
                    lo, hi = half * 64, half * 64 + 64
                    kin = khs[s_][lo:hi, :].rearrange("p (r c) -> p r c", c=64)
                    P.add("dve", lambda e, lo=lo, hi=hi, kin=kin: e.tensor_copy(out=Kbds[s_][lo:hi, :, lo:hi], in_=kin),
                          r=[("kh", s_)], w=[("Kbd", s_, half)])
                return
            P.add("sp", lambda e: e.dma_start(out=khs[s_], in_=kT_d.ap()[hp * 128:(hp + 1) * 128, :]),
                  r=[("kT", hp * 128)], w=[("kh", s_)], dma=True)
            P.add("sp", lambda e: e.dma_start(out=qhs[s_], in_=qT_d.ap()[hp * 128:(hp + 1) * 128, :]),
                  r=[("qT", hp * 128)], w=[("qh", s_)], dma=True)
            P.add("sp", lambda e: e.dma_start(out=tabs[s_], in_=tab_in.ap()[l, hp]), w=[("tabf", s_)], dma=True)
            if part == "all":
                P.add("dve", lambda e: e.tensor_scalar(out=tabb[s_], in0=tabs[s_], scalar1=8.0, scalar2=None, op0=ALU.mult),
                      r=[("tabf", s_)], w=[("tab", s_)])
            for half in range(2):
                lo, hi = half * 64, half * 64 + 64
                src = vtok_d.ap()[:, hp * 128 + lo:hp * 128 + hi].rearrange("(r c) d -> c r d", c=64)
                vq = "pool" if (hp > 0 or half == 0) else "sp"
                for rq in range(4):
                    P.add(vq, lambda e, lo=lo, hi=hi, src=src, rq=rq: e.dma_start(
                        out=Vbds[s_][lo:hi, rq * 16:(rq + 1) * 16, lo:hi], in_=src[:, rq * 16:(rq + 1) * 16, :]),
                        r=[("vtok", i) for i in range(8 * rq, 8 * rq + 8)], w=[("Vbd", s_, half)], dma=True)
                kin = khs[s_][lo:hi, :].rearrange("p (r c) -> p r c", c=64)
                if hp == 0 and half == 0:
                    P.add("dve", lambda e, lo=lo, hi=hi, kin=kin: e.tensor_copy(out=Kbds[s_][lo:hi, :, lo:hi], in_=kin),
                          r=[("kh", s_)], w=[("Kbd", s_, half)])
                elif hp == 0:
                    P.add("act", lambda e, lo=lo, hi=hi, kin=kin: e.activation(out=Kbds[s_][lo:hi, :, lo:hi], in_=kin, func=AF.Copy),
                          r=[("kh", s_)], w=[("Kbd", s_, half)])
                elif part == "all":
                    P.add("dve", lambda e, lo=lo, hi=hi, kin=kin: e.tensor_copy(out=Kbds[s_][lo:hi, :, lo:hi], in_=kin),
                          r=[("kh", s_)], w=[("Kbd", s_, half)])
            P.add("sp", lambda e: e.dma_start(out=guhs[s_], in_=gu_d.ap()[hp * 128:(hp + 1) * 128, :]),
                  r=[("gu", hp * 128)], w=[("guh", s_)], dma=True)

        load_hp(0)
        zero_offdiag(1)
        sgu_loads(0)
        for hp in range(4):
            s_ = hp % 2
            qh, guh, yst, Kbd, Vbd, tab = qhs[s_], guhs[s_], ysts[s_], Kbds[s_], Vbds[s_], tabb[s_]
            kKbd = [("Kbd", s_, 0), ("Kbd", s_, 1), ("KbdZ", s_, 0), ("KbdZ", s_, 1)]
            kVbd = [("Vbd", s_, 0), ("Vbd", s_, 1), ("VbdZ", s_, 0), ("VbdZ", s_, 1)]
            kqh, ktab, kguh, kyst = ("qh", s_), ("tab", s_), ("guh", s_), ("yst", s_)
            if hp + 1 < 4:
                load_hp(hp + 1, part="dma")
            work = []
            for j in range(8):
                r0 = 8 * j
                pieces = []
                for kr in range(64):
                    rr = [r for r in range(r0, r0 + 8) if rs_[r] <= kr <= rs_[r] + 7]
                    if rr:
                        pieces.append((kr, rr[0], rr[-1]))
                pieces.sort(key=lambda p_: -(p_[2] - p_[1]))
                assert pieces[0][1] == r0 and pieces[0][2] == r0 + 7
                for pi, (kr, ra, rb) in enumerate(pieces):
                    work.append((j, r0, kr, ra, rb, pi == 0, pi == len(pieces) - 1))
            LOOK = 3
            NB = 6
            for t in range(len(work) + LOOK):
                if t == 34 and hp + 1 < 4:
                    load_hp(hp + 1, part="fill")
                if t >= 16 and t % 12 == 4 and (t - 16) // 12 < 8:
                    sgu_bank(hp, (t - 16) // 12)
                    if (t - 16) // 12 == 7 and hp + 1 < 4:
                        sgu_loads(hp + 1)
                if t < len(work):
                    j, r0, kr, ra, rb, first, lastp = work[t]
                    n64 = (rb - ra + 1) * 64
                    i0 = 7 - kr + ra
                    assert 0 <= i0 and i0 + (rb - ra) <= 14
                    sbk = zbank()
                    b2 = t % NB
                    def smm4(e, sbk=sbk, kr=kr, ra=ra, rb=rb, n64=n64, Kbd=Kbd, qh=qh, tab=tab, i0=i0):
                        e.matmul(PS[sbk][:, 0:n64], lhsT=Kbd[:, kr, :], rhs=qh[:, ra * 64:(rb + 1) * 64], start=True, stop=False)
                        return e.matmul(PS[sbk][:, 0:n64], lhsT=ident, rhs=tab[:, i0 * 64:i0 * 64 + n64], start=False, stop=True)
                    P.add("pe", smm4, r=kKbd + [kqh, ktab, "const"], w=[("ps", sbk)])
                    P.add("act", lambda e, sbk=sbk, b2=b2, n64=n64: e.activation(out=Pt[b2][:, 0:n64], in_=PS[sbk][:, 0:n64],
                                                                                 func=AF.Exp, scale=0.125),
                          r=[("ps", sbk)], w=[("Pt", b2)])
                if t >= LOOK:
                    u = t - LOOK
                    j, r0, kr, ra, rb, first, lastp = work[u]
                    n64 = (rb - ra + 1) * 64
                    b2 = u % NB
                    yb, db = 4 + (j % 2), 6 + (j % 2)
                    c0, c1 = (ra - r0) * 64, (rb - r0 + 1) * 64

                    def pv(e, yb=yb, db=db, kr=kr, b2=b2, n64=n64, c0=c0, c1=c1, first=first, lastp=lastp, Vbd=Vbd):
                        e.matmul(PS[yb][:, c0:c1], lhsT=Vbd[:, kr, :], rhs=Pt[b2][:, 0:n64], start=first, stop=lastp,
                                 skip_group_check=True)
                        return e.matmul(PS[db][:, c0:c1], lhsT=onesbd, rhs=Pt[b2][:, 0:n64], start=first, stop=lastp,
                                        skip_group_check=True)
                    P.add("pe", pv, r=kVbd + [("Pt", b2), "const"], w=[("ps", yb), ("ps", db)])
                    if lastp:
                        rc = recs[j % 2]
                        P.add("act", lambda e, db=db, rc=rc: e.activation(out=rc, in_=PS[db], func=AF.Ln), r=[("ps", db)], w=[("rec", j % 2)])
                        P.add("act", lambda e, rc=rc: e.activation(out=rc, in_=rc, func=AF.Exp, scale=-1.0),
                              r=[("rec", j % 2)], w=[("rec", j % 2)])
                        P.add("dve", lambda e, yb=yb, rc=rc: e.tensor_tensor(out=rc, in0=PS[yb], in1=rc, op=ALU.mult),
                              r=[("ps", yb), ("rec", j % 2)], w=[("rec", j % 2)])
                        P.add("dve", lambda e, r0=r0, rc=rc, yst=yst, guh=guh: e.tensor_tensor(
                            out=yst[:, r0 * 64:(r0 + 8) * 64], in0=rc, in1=guh[:, r0 * 64:(r0 + 8) * 64], op=ALU.mult),
                              r=[("rec", j % 2), kguh], w=[kyst])
            P.add("sp", lambda e, hp=hp, yst=yst: e.dma_start(out=ys_d.ap()[:, :, 4 + hp, :].rearrange("b p t -> p b t"),
                                                              in_=yst.rearrange("p (b t) -> p b t", b=8)),
                  r=[kyst], w=[("ys", W + hp * 128)], dma=True)
        if upto <= 4:
            break

        barrier()
        psn[0] = 8
        A5 = Arena(BASE)
        hcps = [A5.bf(T + 32) for _ in range(4)]
        diags = [A5.bf(31 * 128).rearrange("p (j c) -> p j c", c=128) for _ in range(4)]
        cws = [A5.f32(31) for _ in range(4)]
        cv = [A5.f32(3) for _ in range(4)]
        hg = [A5.f32(1) for _ in range(4)]
        hbv = [A5.f32(1) for _ in range(4)]
        cxbs = [A5.bf(4 * 512).rearrange("p (g t) -> p g t", g=4) for _ in range(2)]
        sqbs = [A5.bf(4 * 512).rearrange("p (g t) -> p g t", g=4) for _ in range(2)]
        gucs = [A5.bf(4 * 512).rearrange("p (g t) -> p g t", g=4) for _ in range(2)]
        st5s = [A5.bf(4 * 512).rearrange("p (g t) -> p g t", g=4) for _ in range(2)]
        mean5 = [A5.f32(512) for _ in range(2)]
        m2s = [A5.f32(512) for _ in range(2)]
        var5 = [A5.f32(512) for _ in range(2)]
        rstd5 = [A5.f32(512) for _ in range(2)]
        t5s = [A5.f32(512) for _ in range(4)]
        y5s = [A5.bf(512) for _ in range(4)]
        th5s = [A5.bf(512) for _ in range(4)]
        assert A5.off <= TOPW, A5.off
        for g in range(4):
            P.add("dve", lambda e, g=g: e.memset(hcps[g][:, 0:16], 0.0), w=[("hcpZ", g)])
            P.add("dve", lambda e, g=g: e.memset(hcps[g][:, 16 + T:32 + T], 0.0), w=[("hcpZ", g)])
        for g in range(4):
            P.add("sp", lambda e, g=g: e.dma_start(out=hcps[g][:, 16:16 + T], in_=hc_d.ap()[g * 128:(g + 1) * 128, :]),
                  r=[("hc", g * 128)], w=[("hcp", g)], dma=True)
            P.add("sp", lambda e, g=g, l=l: e.dma_start(out=cws[g], in_=convw_in.ap()[l, g * 128:(g + 1) * 128, :]), w=[("cw", g)], dma=True)
            P.add("sp", lambda e, g=g, l=l: e.dma_start(out=cv[g], in_=cvec_in.ap()[l, g * 128:(g + 1) * 128, :]), w=[("cv", g)], dma=True)
        for g in range(4):
            P.add("dve", lambda e, g=g: e.tensor_scalar(out=hg[g], in0=cv[g][:, 1:2], scalar1=0.5, scalar2=None, op0=ALU.mult),
                  r=[("cv", g)], w=[("hg", g)])
            P.add("dve", lambda e, g=g: e.tensor_scalar(out=hbv[g], in0=cv[g][:, 2:3], scalar1=0.5, scalar2=None, op0=ALU.mult),
                  r=[("cv", g)], w=[("hbv", g)])
            for jj in range(31):
                if jj % 2 == 0:
                    P.add("dve", lambda e, jj=jj, g=g: e.tensor_scalar(out=diags[g][:, jj, :], in0=ident, scalar1=cws[g][:, jj:jj + 1],
                                                                       scalar2=None, op0=ALU.mult),
                          r=[("cw", g), "const"], w=[("diag", g, jj)])
                else:
                    P.add("act", lambda e, jj=jj, g=g: e.activation(out=diags[g][:, jj, :], in_=ident, func=AF.Copy,
                                                                    scale=cws[g][:, jj:jj + 1]),
                          r=[("cw", g), "const"], w=[("diag", g, jj)])
        P.add("pool", lambda e, l=l: e.dma_start(out=wbr, in_=wbr_in.ap()[l].rearrange("(j p) d -> p j d", p=128)), w=["wbr"], dma=True)
        P.add("pool", lambda e, l=l: e.dma_start(out=wo, in_=wout_in.ap()[l].rearrange("(j p) d -> p j d", p=128)), w=["wo"], dma=True)

        def conv_epi(b):
            q = b % 2
            bs = slice(b * 512, (b + 1) * 512)
            cx, sqb, guc, st5 = cxbs[q], sqbs[q], gucs[q], st5s[q]
            kcx = [("cxb", q, g) for g in range(4)]
            P.add("pool", lambda e: e.tensor_tensor(out=sqb, in0=cx, in1=cx, op=ALU.mult), r=kcx, w=[("sqb", q)])
            mb, vb_ = zbank(), zbank()

            def stat(e):
                for g in range(4):
                    e.matmul(PS[mb], lhsT=ones512, rhs=cx[:, g, :], start=(g == 0), stop=(g == 3))
                for g in range(4):
                    ins = e.matmul(PS[vb_], lhsT=ones512, rhs=sqb[:, g, :], start=(g == 0), stop=(g == 3))
                return ins
            P.add("pe", stat, r=kcx + [("sqb", q), "const"], w=[("ps", mb), ("ps", vb_)])
            P.add("dve", lambda e: e.tensor_copy(out=mean5[q], in_=PS[mb]), r=[("ps", mb)], w=[("mean5", q)])
            P.add("dve", lambda e: e.tensor_tensor(out=m2s[q], in0=mean5[q], in1=mean5[q], op=ALU.mult),
                  r=[("mean5", q)], w=[("m2", q)])
            P.add("dve", lambda e: e.tensor_tensor(out=var5[q], in0=PS[vb_], in1=m2s[q], op=ALU.subtract),
                  r=[("ps", vb_), ("m2", q)], w=[("var5", q)])
            P.add("act", lambda e: e.activation(out=var5[q], in_=var5[q], func=AF.Ln, bias=eps_t), r=[("var5", q), "const"], w=[("var5", q)])
            P.add("act", lambda e: e.activation(out=rstd5[q], in_=var5[q], func=AF.Exp, scale=-0.5), r=[("var5", q)], w=[("rstd5", q)])
            for g in range(4):
                P.add("pool", lambda e, g=g: e.tensor_tensor(out=t5s[g], in0=cx[:, g, :], in1=mean5[q], op=ALU.subtract),
                      r=[("cxb", q, g), ("mean5", q)], w=[("t5", g)])
            for g in range(4):
                P.add("dve", lambda e, g=g: e.tensor_tensor(out=t5s[g], in0=t5s[g], in1=rstd5[q], op=ALU.mult),
                      r=[("t5", g), ("rstd5", q)], w=[("t5", g)])
            for g in range(4):
                P.add("act", lambda e, g=g: e.activation(out=th5s[g], in_=t5s[g], func=AF.Tanh, scale=hg[g], bias=hbv[g]),
                      r=[("t5", g), ("hg", g), ("hbv", g)], w=[("th5", g)])
                P.add("act", lambda e, g=g: e.activation(out=y5s[g], in_=t5s[g], func=AF.Identity, scale=hg[g], bias=hbv[g]),
                      r=[("t5", g), ("hg", g), ("hbv", g)], w=[("y5", g)])
            for g in range(4):
                P.add("dve", lambda e, g=g: e.scalar_tensor_tensor(out=y5s[g], in0=th5s[g], scalar=1.0, in1=y5s[g],
                                                                   op0=ALU.add, op1=ALU.mult),
                      r=[("th5", g), ("y5", g)], w=[("y5", g)])
            for g in range(4):
                P.add("dve", lambda e, g=g: e.tensor_tensor(out=st5[:, g, :], in0=y5s[g], in1=guc[:, g, :], op=ALU.mult),
                      r=[("y5", g), ("guc", q)], w=[("st5", q)])
            P.add("sp", lambda e: e.dma_start(out=ys_d.ap()[b, :, 8:12, :], in_=st5),
                  r=[("st5", q)], w=[("ys", 2 * W + g_ * 128) for g_ in range(4)], dma=True)

        for blk in range(9):
            if blk < 8:
                q = blk % 2
                bs = slice(blk * 512, (blk + 1) * 512)
                P.add("sp", lambda e, q=q, bs=bs: e.dma_start(out=gucs[q], in_=gu_d.ap()[W:2 * W, bs].rearrange("(g c) t -> c g t", c=128)),
                      r=[("gu", W + g_ * 128) for g_ in range(4)], w=[("guc", q)], dma=True)
                for g in range(4):
                    bank = zbank()

                    def cmm(e, bank=bank, blk=blk, g=g):
                        for jj in range(31):
                            o = 16 + blk * 512 + jj - 15
                            ins = e.matmul(PS[bank], lhsT=diags[g][:, jj, :], rhs=hcps[g][:, o:o + 512], start=(jj == 0), stop=(jj == 30))
                        return ins
                    P.add("pe", cmm, r=[("diag", g, jj) for jj in range(31)] + [("hcp", g), ("hcpZ", g)], w=[("ps", bank)])
                    P.add("act", lambda e, g=g, q=q, bank=bank: e.activation(
                        out=cxbs[q][:, g, :], in_=PS[bank], func=AF.Identity, scale=1.0, bias=cv[g][:, 0:1]),
                        r=[("ps", bank), ("cv", g)], w=[("cxb", q, g)])
            if blk >= 1:
                conv_epi(blk - 1)
        if upto <= 5:
            break

        barrier()
        A7 = Arena(BASE)
        ysbs = [A7.bf(16 * 512).rearrange("p (j t) -> p j t", j=16) for _ in range(2)]
        thms = [thm0, A7.bf(32 * 512).rearrange("p (j t) -> p j t", j=32)]
        mTs = [A7.bf(8 * 512).rearrange("p (j t) -> p j t", j=8) for _ in range(2)]
        tns = [[A7.bf(512) for _ in range(4)] for _ in range(2)]
        pbfs = [A7.bf(512) for _ in range(6)]
        xt7 = [A7.f32(D) for _ in range(3)]
        ot7 = xt7
        sq7s = [A7.bf(D) for _ in range(2)]
        ms7s = [A7.f32(1) for _ in range(3)]
        rs7s = [A7.f32(1) for _ in range(3)]
        fg_bc = A7.f32(D)
        assert A7.off <= TOP0, (A7.off, TOP0)
        P.add("sp", lambda e: e.dma_start(out=fg_bc, in_=fg_in.ap()), w=["fg_bc"], dma=True)
        ys_all = [("ys", r_) for r_ in range(0, 4 * W, 128)]

        def load7(blk, only_thm=False, skip_thm=False):
            bs = slice(blk * 512, (blk + 1) * 512)
            ysb, thm = ysbs[blk % 2], thms[blk % 2]
            if not only_thm:
                for hf in range(2):
                    P.add("act", lambda e, hf=hf: e.dma_start(
                        out=ysb[:, hf * 8:(hf + 1) * 8, :],
                        in_=ys_d.ap()[blk, :, hf * 8:(hf + 1) * 8, :]),
                        r=ys_all[hf * 8:(hf + 1) * 8], w=[("ysb", blk % 2, hf)], dma=True)
            if not skip_thm:
                for qt in range(4):
                    P.add("sp", lambda e, qt=qt: e.dma_start(
                        out=thm[:, qt * 8:(qt + 1) * 8, :],
                        in_=thm_d.ap()[blk, :, qt * 8:(qt + 1) * 8, :]),
                        r=[("thm", r_) for r_ in range(qt * 1024, (qt + 1) * 1024, 128)], w=[("thm", blk % 2, qt)], dma=True)

        def outproj_a(blk, tt, l=l, xs=xs):
            mT = mTs[blk % 2]
            if True:
                i = blk * 4 + tt
                b = i % 3
                P.add("sp", lambda e, i=i, b=b: e.dma_start(out=xt7[b], in_=xs[i * 128:(i + 1) * 128, :]),
                      r=[("x", l)], w=[("xt7", b)], dma=True)
                for hf in range(2):
                    bank = zbank()

                    def omm(e, bank=bank, tt=tt, hf=hf):
                        for kd in range(8):
                            ins = e.matmul(PS[bank], lhsT=mT[:, kd, tt * 128:(tt + 1) * 128], rhs=wo[:, kd, hf * 512:(hf + 1) * 512],
                                           start=(kd == 0), stop=(kd == 7))
                        return ins
                    P.add("pe", omm, r=[("mT", blk % 2), "wo"], w=[("ps", bank)])
                    P.add("dve", lambda e, bank=bank, b=b, hf=hf: e.scalar_tensor_tensor(
                        out=ot7[b][:, hf * 512:(hf + 1) * 512], in0=PS[bank], scalar=1.0, in1=xt7[b][:, hf * 512:(hf + 1) * 512],
                        op0=ALU.mult, op1=ALU.add), r=[("ps", bank), ("xt7", b)], w=[("xt7", b), ("ot7", b)])

        def outproj_b(blk, tt, l=l, xdst=xdst, last=last):
            if True:
                i = blk * 4 + tt
                b = i % 3
                if last:
                    sq7, ms7, rs7 = sq7s[i % 2], ms7s[b], rs7s[b]
                    P.add("act", lambda e, b=b, sq7=sq7, ms7=ms7: e.activation(out=sq7, in_=ot7[b], func=AF.Square, scale=1.0 / 32.0,
                                                                               accum_out=ms7),
                          r=[("ot7", b)], w=[("sq7", i % 2), ("ms7", b)])
                    P.add("act", lambda e, ms7=ms7: e.activation(out=ms7, in_=ms7, func=AF.Ln, bias=eps_t),
                          r=[("ms7", b), "const"], w=[("ms7", b)])
                    P.add("act", lambda e, ms7=ms7, rs7=rs7: e.activation(out=rs7, in_=ms7, func=AF.Exp, scale=-0.5),
                          r=[("ms7", b)], w=[("rs7", b)])
                    P.add("act", lambda e, b=b, rs7=rs7: e.activation(out=ot7[b], in_=ot7[b], func=AF.Identity, scale=rs7),
                          r=[("ot7", b), ("rs7", b)], w=[("ot7", b)])
                    P.add("pool", lambda e, b=b: e.tensor_tensor(out=ot7[b], in0=ot7[b], in1=fg_bc, op=ALU.mult),
                          r=[("ot7", b), "fg_bc"], w=[("ot7", b)])
                P.add("pool", lambda e, i=i, b=b: e.dma_start(out=xdst.ap()[i * 128:(i + 1) * 128, :], in_=ot7[b]),
                      r=[("ot7", b), ("xt7", b)], w=[("x", l + 1), ("xrow", l, i)], dma=True)

        load7(0)
        for blk in range(8):
            ysb, thm, mT = ysbs[blk % 2], thms[blk % 2], mTs[blk % 2]
            if blk + 1 < 8:
                load7(blk + 1)
            for dg in range(8):
                tn = tns[dg % 2]
                for n in range(4):
                    bank = zbank()

                    def pmm(e, bank=bank, n=n, dg=dg, ysb=ysb):
                        for kc in range(4):
                            ins = e.matmul(PS[bank], lhsT=wbr[:, n * 4 + kc, dg * 128:(dg + 1) * 128], rhs=ysb[:, n * 4 + kc, :],
                                           start=(kc == 0), stop=(kc == 3))
                        return ins
                    P.add("pe", pmm, r=["wbr", ("ysb", blk % 2, n // 2)], w=[("ps", bank)])
                    pi_ = (dg * 4 + n) % 6
                    P.add("act", lambda e, bank=bank, pi_=pi_: e.activation(out=pbfs[pi_], in_=PS[bank], func=AF.Copy),
                          r=[("ps", bank)], w=[("pbf", pi_)])
                    P.add("dve", lambda e, pi_=pi_, n=n, dg=dg, tn=tn, thm=thm: e.tensor_tensor(
                        out=tn[n], in0=thm[:, n * 8 + dg, :], in1=pbfs[pi_], op=ALU.mult),
                        r=[("thm", blk % 2, n), ("pbf", pi_)], w=[("tn", dg % 2, n)])
                P.add("dve", lambda e, tn=tn: e.tensor_tensor(out=tn[2], in0=tn[2], in1=tn[3], op=ALU.add),
                      r=[("tn", dg % 2, 2), ("tn", dg % 2, 3)], w=[("tn", dg % 2, 2)])
                P.add("dve", lambda e, tn=tn: e.tensor_tensor(out=tn[0], in0=tn[0], in1=tn[1], op=ALU.add),
                      r=[("tn", dg % 2, 0), ("tn", dg % 2, 1)], w=[("tn", dg % 2, 0)])
                P.add("dve", lambda e, tn=tn, dg=dg, mT=mT: e.tensor_tensor(out=mT[:, dg, :], in0=tn[0], in1=tn[2], op=ALU.add),
                      r=[("tn", dg % 2, 0), ("tn", dg % 2, 2)], w=[("mT", blk % 2)])
                if dg == 3 and blk > 0:
                    for tt in range(4):
                        outproj_a(blk - 1, tt)
                        outproj_b(blk - 1, tt)
        for tt in range(4):
            outproj_a(7, tt)
            outproj_b(7, tt)

    P.emit(nc, stack)
    stack.close()
    return nc


_CONSTS = None


def make_in_maps(inputs):
    global _CONSTS
    if _CONSTS is None:
        _CONSTS = host_consts()
    c = _CONSTS
    f = lambda a: np.ascontiguousarray(np.asarray(a, dtype=np.float32))
    x = f(inputs["x"])
    bc = lambda v: np.ascontiguousarray(np.broadcast_to(v[:, None, :], (v.shape[0], 128, v.shape[1])))
    shared = {
        "norm_g_bc": bc(f(inputs["norm_g"])),
        "final_g_bc": np.ascontiguousarray(np.broadcast_to(f(inputs["final_g"])[None, :], (128, D))),
        "w_in": f(inputs["w_in"]),
        "w_branch": f(inputs["w_branch"]).reshape(L, 4 * W, D),
        "w_out": f(inputs["w_out"]),
        "attn_tab": attn_tables(f(inputs["nat_rpb"])),
        "conv_wT": np.ascontiguousarray(f(inputs["conv_w"]).transpose(0, 2, 1)),
        "cvec": np.ascontiguousarray(np.stack([f(inputs["conv_b"]), f(inputs["conv_ln_g"]),
                                               f(inputs["conv_ln_b"])], axis=-1)),
        "sgu_ln_g_bc": bc(f(inputs["sgu_ln_g"])),
        "sgu_ln_b_bc": bc(f(inputs["sgu_ln_b"])),
        "sgu_wT": np.ascontiguousarray(f(inputs["sgu_w"]).transpose(0, 3, 1, 2)),
        "sgu_b_bc": np.ascontiguousarray(np.broadcast_to(f(inputs["sgu_b"])[:, None, :, None, :], (L, 128, 4, 4, 128))).reshape(L, 128, 4, 512),
        "c_ident": c["ident"], "c_cs128": c["cs128"], "c_f1": c["f1"], "c_dft": c["dft"],
        "c_onesbd": c["onesbd"], "c_ones512": c["ones512"], "c_neghalf": c["neghalf"],
    }
    maps = []
    for b in range(8):
        m = dict(shared)
        m["x"] = np.ascontiguousarray(x[b])
        maps.append(m)
    return maps


def kernel(**inputs):
    nc = build()
    res = run_bass_kernel_spmd(nc, make_in_maps(inputs), core_ids=list(range(8)))
    return np.stack([np.asarray(r["out"], dtype=np.float32) for r in res.results], axis=0)
```

```python
import contextlib
import numpy as np
import ml_dtypes
import concourse.bass as bass
import concourse.mybir as mybir
from concourse.bass_utils import run_bass_kernel_spmd

F32 = mybir.dt.float32
BF16 = mybir.dt.bfloat16
AF = mybir.ActivationFunctionType
ALU = mybir.AluOpType
AX = mybir.AxisListType
NPBF = ml_dtypes.bfloat16

T = 4096
D = 1024
W = 512
L = 2
EPS = 1e-6
COMPUTE = ("pe", "act", "dve", "pool")


class Op:
    __slots__ = ("eng", "fn", "deps", "dma", "sem", "val", "need_inc", "prev_same_sem")


class Prog:
    def __init__(self):
        self.ops = []
        self.lastw = {}
        self.rd_eng = {}
        self.rd_dma = {}

    def add(self, eng, fn, r=(), w=(), dma=False, barrier=False):
        op = Op()
        op.eng, op.fn, op.dma = eng, fn, dma
        op.sem = None
        op.val = 0
        op.need_inc = False
        op.prev_same_sem = None
        r = tuple(r)
        w = tuple(w)
        if barrier:
            w = w + ("PHASE",)
        else:
            r = r + ("PHASE",)
        deps = []
        for k in r:
            lw = self.lastw.get(k)
            if lw is not None:
                deps.append(lw)
        for k in w:
            lw = self.lastw.get(k)
            if lw is not None:
                deps.append(lw)
            deps.extend(self.rd_eng.get(k, {}).values())
            deps.extend(self.rd_dma.get(k, []))
        for k in r:
            if dma:
                self.rd_dma.setdefault(k, []).append(op)
            else:
                self.rd_eng.setdefault(k, {})[eng] = op
        for k in w:
            self.lastw[k] = op
            self.rd_eng[k] = {}
            self.rd_dma[k] = []
        seen = set()
        op.deps = []
        for d in deps:
            if d is not op and id(d) not in seen:
                seen.add(id(d))
                op.deps.append(d)
        self.ops.append(op)
        return op

    def emit(self, nc, stack):
        ndma = {"sp": 8, "pool": 10, "act": 6}
        for op in self.ops:
            for d in op.deps:
                if d.dma:
                    continue
                if d.eng != op.eng or op.dma or d.eng != "pe":
                    d.need_inc = True
        sem_eng = {e: stack.enter_context(nc.semaphore("c_" + e)) for e in COMPUTE}
        cnt = {e: 0 for e in COMPUTE}
        dsem, dcnt, dlast, drr = {}, {}, {}, {}
        for q, n in ndma.items():
            dsem[q] = [stack.enter_context(nc.semaphore("d_%s%d" % (q, i))) for i in range(n)]
            dcnt[q] = [0] * n
            dlast[q] = [None] * n
            drr[q] = 0
        for op in self.ops:
            if op.dma:
                q = op.eng
                i = drr[q]
                drr[q] = (i + 1) % len(dsem[q])
                dcnt[q][i] += 16
                op.sem = dsem[q][i]
                op.val = dcnt[q][i]
                op.prev_same_sem = dlast[q][i]
                dlast[q][i] = op
            else:
                op.sem = sem_eng[op.eng]
                if op.need_inc:
                    cnt[op.eng] += 1
                op.val = cnt[op.eng]
        per = {}
        for op in self.ops:
            per.setdefault(op.eng, []).append(op)
        all_dma = [op for op in self.ops if op.dma]

        def run(eng_name, e):
            waited = {}

            def wait(sem, val):
                key = id(sem)
                if waited.get(key, 0) < val:
                    e.wait_ge(sem, val)
                    waited[key] = val

            for op in per.get(eng_name, []):
                for d in op.deps:
                    if (not d.dma) and d.eng == op.eng and (not op.dma) and d.eng == "pe":
                        continue
                    wait(d.sem, d.val)
                if op.dma and op.prev_same_sem is not None:
                    wait(op.prev_same_sem.sem, op.prev_same_sem.val)
                inst = op.fn(e)
                if op.dma:
                    inst.then_inc(op.sem, 16)
                elif op.need_inc:
                    inst.then_inc(op.sem, 1)
            if eng_name == "sp":
                fin = {}
                for op in all_dma:
                    fin[id(op.sem)] = (op.sem, max(fin.get(id(op.sem), (None, 0))[1], op.val))
                for sem, val in fin.values():
                    wait(sem, val)

        with nc.Block() as block:
            @block.tensor
            def _(e):
                run("pe", e)

            @block.scalar
            def _(e):
                run("act", e)

            @block.vector
            def _(e):
                run("dve", e)

            @block.gpsimd
            def _(e):
                run("pool", e)

            @block.sync
            def _(e):
                run("sp", e)


def host_consts():
    c = {}
    c["ident"] = np.eye(128, dtype=np.float32).astype(NPBF)
    m = np.arange(128)
    ang = 2 * np.pi * np.outer(m, m) / 128.0
    s128 = 1.0 / np.sqrt(128.0)
    c["cs128"] = np.concatenate([np.cos(ang) * s128, np.sin(ang) * s128], axis=1).astype(NPBF)
    t1 = np.arange(64)
    phi = 2 * np.pi * np.outer(t1, t1) / 64.0
    C, S = np.cos(phi) / 8.0, np.sin(phi) / 8.0
    f1 = np.zeros((128, 128), np.float64)
    f1[0:64, 0:64] = C
    f1[64:128, 0:64] = -S
    f1[0:64, 64:128] = S
    f1[64:128, 64:128] = C
    c["f1"] = f1.astype(NPBF)
    t2 = np.arange(64)[:, None, None]
    k1 = np.arange(64)[None, :, None]
    k2 = np.arange(64)[None, None, :]
    th = 2 * np.pi * t2 * (k1 + 64 * k2) / 4096.0
    dft = np.concatenate([np.cos(th) / 8.0, -np.sin(th) / 8.0], axis=0)
    c["dft"] = dft.reshape(128, 4096).astype(NPBF)
    obd = np.zeros((128, 128), np.float32)
    obd[0:64, 0:64] = 1
    obd[64:, 64:] = 1
    c["onesbd"] = obd.astype(NPBF)
    c["ones512"] = np.full((128, 128), 1.0 / 512.0, np.float32).astype(NPBF)
    c["neghalf"] = np.full((128, 1), -0.5, np.float32)
    return c


def attn_tables(rpb):
    kc = np.arange(64)[:, None]
    qc = np.arange(64)[None, :]
    cs = np.clip(qc - 8, 0, 48)
    inwin = (kc >= cs) & (kc < cs + 16)
    dc = np.clip(kc - qc + 15, 0, 30)
    g = rpb[:, :, ::-1, :][:, :, :, dc]
    g = np.where(inwin[None, None, None], g, np.float32(-30000.0))
    g = g.transpose(0, 1, 3, 2, 4)
    g = g.reshape(L, 4, 2, 64, 15, 64).reshape(L, 4, 128, 15 * 64)
    return np.ascontiguousarray(g.astype(np.float32))


def build(debug=None, nlayers=L, upto=99):
    nc = bass.Bass("TRN2", target_bir_lowering=False)
    P = Prog()

    def din(name, shape, dt=F32):
        return nc.dram_tensor(name, list(shape), dt, kind="ExternalInput")

    x_in = din("x", [T, D])
    ng_in = din("norm_g_bc", [L, 128, D])
    fg_in = din("final_g_bc", [128, D])
    win_in = din("w_in", [L, D, 10240])
    wbr_in = din("w_branch", [L, 4 * W, D])
    wout_in = din("w_out", [L, D, D])
    tab_in = din("attn_tab", [L, 4, 128, 960])
    convw_in = din("conv_wT", [L, W, 31])
    cvec_in = din("cvec", [L, W, 3])
    sgu_lng_in = din("sgu_ln_g_bc", [L, 128, W])
    sgu_lnb_in = din("sgu_ln_b_bc", [L, 128, W])
    sgu_wT_in = din("sgu_wT", [L, 128, 4, 128])
    sgu_b_in = din("sgu_b_bc", [L, 128, 4, 512])
    c_ident = din("c_ident", [128, 128], BF16)
    c_cs128 = din("c_cs128", [128, 256], BF16)
    c_f1 = din("c_f1", [128, 128], BF16)
    c_dft = din("c_dft", [128, 4096], BF16)
    c_onesbd = din("c_onesbd", [128, 128], BF16)
    c_ones512 = din("c_ones512", [128, 128], BF16)
    c_neghalf = din("c_neghalf", [128, 1])
    out_t = nc.dram_tensor("out", [T, D], F32, kind="ExternalOutput")

    def scratch(name, shape, dt=BF16):
        kind = "ExternalOutput" if (debug and (name in debug or (name == "yT" and "aT" in debug))) else "Internal"
        return nc.dram_tensor(name, list(shape), dt, kind=kind)

    qT_d = scratch("qT", [W, T])
    kT_d = scratch("kT", [W, T])
    vtok_d = scratch("vtok", [T, W])
    hc_d = scratch("hc", [W, T])
    uT_d = scratch("uT", [W, T])
    vn_d = scratch("vn", [T, W])
    gu_d = scratch("gu", [3 * W, T])
    thm_d = scratch("thm", [8, 128, 32, 512])
    ys_d = scratch("ys", [8, 128, 16, 512])
    x1_d = scratch("x1", [T, D], F32)
    yT_d = scratch("yT", [W, T])

    stack = contextlib.ExitStack()
    NBF = 103 * 1024
    SB = stack.enter_context(nc.sbuf_tensor("sb", [128, NBF], BF16))
    SBF = SB.bitcast(F32)
    PSH = [stack.enter_context(nc.psum_tensor("ps%d" % i, [128, 512], F32)) for i in range(8)]
    PS = [p[:] for p in PSH]
    PSB = [p.bitcast(BF16)[:] for p in PSH]

    class Arena:
        def __init__(self, base=0):
            self.off = base

        def bf(self, n):
            o = self.off
            self.off += (n + 31) // 32 * 32
            assert self.off <= NBF, self.off
            return SB[:, o:o + n]

        def f32(self, n):
            o = self.off
            self.off += (2 * n + 31) // 32 * 32
            assert self.off <= NBF, self.off
            return SBF[:, o // 2:o // 2 + n]

    def barrier():
        P.add("dve", lambda e: e.memset(bar_t, 0.0), barrier=True)

    A0 = Arena()
    ident = A0.bf(128)
    cs128 = A0.bf(256)
    f1 = A0.bf(128)
    onesbd = A0.bf(128)
    ones512 = A0.bf(128)
    neghalf = A0.f32(1)
    bar_t = A0.f32(8)
    eps_t = A0.f32(1)
    dft = A0.bf(4096)
    for ap, src in ((ident, c_ident), (cs128, c_cs128), (f1, c_f1), (onesbd, c_onesbd),
                    (ones512, c_ones512), (neghalf, c_neghalf), (dft, c_dft)):
        P.add("sp", lambda e, a=ap, s=src: e.dma_start(out=a, in_=s.ap()), w=["const"], dma=True)
    P.add("dve", lambda e: e.memset(eps_t, EPS), w=["const"])
    BASE = A0.off
    TOP0 = NBF - (16 * D + 8 * D + 32 * 512)
    TOPW = NBF - (16 * D + 8 * D)
    _AT = Arena(TOP0)
    thm0 = _AT.bf(32 * 512).rearrange("p (j t) -> p j t", j=32)
    wbr = _AT.bf(16 * D).rearrange("p (j d) -> p j d", j=16)
    wo = _AT.bf(8 * D).rearrange("p (j d) -> p j d", j=8)

    psrr = [0]

    psn = [4]

    def zbank():
        b = psrr[0] % psn[0]
        psrr[0] = (b + 1) % psn[0]
        return b

    for l in range(nlayers):
        xsrc = x_in if l == 0 else x1_d
        xdst = x1_d if l < L - 1 else out_t
        last = (l == L - 1)
        barrier()
        psn[0] = 4
        A1 = Arena(BASE)
        hT = A1.bf(8 * T).rearrange("p (k t) -> p k t", k=8)
        aT = A1.bf(4 * T).rearrange("p (g t) -> p g t", g=4)
        wbuf = [A1.bf(8 * 512).rearrange("p (k c) -> p k c", k=8) for _ in range(2)]
        g_bc = A1.f32(D)
        SH = A1.off
        xt = [A1.f32(D) for _ in range(8)]
        sqs = [A1.f32(D) for _ in range(4)]
        hb = [A1.bf(D) for _ in range(8)]
        ms = [A1.f32(1) for _ in range(8)]
        rstd = [A1.f32(1) for _ in range(8)]

        P.add("sp", lambda e, l=l: e.dma_start(out=g_bc, in_=ng_in.ap()[l]), w=["g_bc"], dma=True)
        wcnt = [0]

        def load_w(c0, ncols=512, l=l):
            b = wcnt[0] % 2
            wcnt[0] += 1
            src = win_in.ap()[l][:, c0:c0 + ncols].rearrange("(k p) c -> p k c", p=128)
            P.add("pool", lambda e, b=b, src=src: e.dma_start(out=wbuf[b][:, :, 0:ncols], in_=src),
                  w=[("wbuf", b)], dma=True)
            return b

        wA = load_w(0)
        xs = xsrc.ap()
        def zmm(e, bank, wb, cg, blk, ncols=128):
            for k in range(8):
                ins = e.matmul(PS[bank][0:ncols, :], lhsT=wbuf[wb][:, k, cg * 128:cg * 128 + ncols],
                               rhs=hT[:, k, blk * 512:(blk + 1) * 512], start=(k == 0), stop=(k == 7))
            return ins

        def zgroup(wb, cg, blk):
            bank = zbank()
            P.add("pe", lambda e: zmm(e, bank, wb, cg, blk),
                  r=[("wbuf", wb)] + [("hT", 4 * blk + j) for j in range(4)], w=[("ps", bank)])
            return bank

        def p1_stats(grp):
            tiles = list(range(grp * 4, grp * 4 + 4))
            for i in tiles:
                b = i % 8
                P.add("sp", lambda e, i=i, b=b, xs=xs: e.dma_start(out=xt[b], in_=xs[i * 128:(i + 1) * 128, :]),
                      r=[("x", l)], w=[("xt", b)], dma=True)
            for i in tiles:
                b = i % 8
                P.add("act", lambda e, b=b, sq=sqs[i % 4]: e.activation(out=sq, in_=xt[b], func=AF.Square, scale=1.0 / 32.0,
                                                                        accum_out=ms[b]),
                      r=[("xt", b)], w=[("sq", i % 4), ("ms", b)])
            for i in tiles:
                b = i % 8
                P.add("act", lambda e, b=b: e.activation(out=ms[b], in_=ms[b], func=AF.Ln, bias=eps_t),
                      r=[("ms", b), "const"], w=[("ms", b)])
            for i in tiles:
                b = i % 8
                P.add("act", lambda e, b=b: e.activation(out=rstd[b], in_=ms[b], func=AF.Exp, scale=-0.5),
                      r=[("ms", b)], w=[("rstd", b)])

        def p1_main(grp):
            tiles = list(range(grp * 4, grp * 4 + 4))
            for i in tiles:
                b = i % 8
                P.add("dve", lambda e, b=b: e.scalar_tensor_tensor(out=hb[b], in0=xt[b], scalar=rstd[b], in1=g_bc,
                                                                   op0=ALU.mult, op1=ALU.mult),
                      r=[("xt", b), ("rstd", b), "g_bc"], w=[("hb", b)])
            for i in tiles:
                b = i % 8
                pb = 4 + (i % 4)

                def tr(e, b=b, pb=pb):
                    for c in range(8):
                        ins = e.transpose(out=PSB[pb][:, c * 128:(c + 1) * 128], in_=hb[b][:, c * 128:(c + 1) * 128],
                                          identity=ident)
                    return ins
                P.add("pe", tr, r=[("hb", b), "const"], w=[("ps", pb)])

        def p1_evac(grp):
            tiles = list(range(grp * 4, grp * 4 + 4))
            for i in tiles:
                pb = 4 + (i % 4)
                if i % 2 == 0:
                    P.add("act", lambda e, i=i, pb=pb: e.activation(
                        out=hT[:, :, i * 128:(i + 1) * 128],
                        in_=PSB[pb][:, 0:1024].rearrange("p (k t) -> p k t", k=8), func=AF.Copy),
                        r=[("ps", pb)], w=[("hT", i)])
                else:
                    P.add("dve", lambda e, i=i, pb=pb: e.tensor_copy(
                        out=hT[:, :, i * 128:(i + 1) * 128],
                        in_=PSB[pb][:, 0:1024].rearrange("p (k t) -> p k t", k=8)),
                        r=[("ps", pb)], w=[("hT", i)])

        def p1_acols(blk):
            for g in range(4):
                bank = zgroup(wA, g, blk)
                if (g + blk) % 2 == 0:
                    P.add("act", lambda e, g=g, blk=blk, bank=bank: e.activation(
                        out=aT[:, g, blk * 512:(blk + 1) * 512], in_=PS[bank], func=AF.Copy),
                        r=[("ps", bank)], w=[("aT", g)])
                else:
                    P.add("dve", lambda e, g=g, blk=blk, bank=bank: e.tensor_copy(
                        out=aT[:, g, blk * 512:(blk + 1) * 512], in_=PS[bank]),
                        r=[("ps", bank)], w=[("aT", g)])

        p1_stats(0)
        for grp in range(8):
            if grp + 1 < 8:
                p1_stats(grp + 1)
            p1_main(grp)
            p1_evac(grp)
            if grp >= 1:
                p1_acols(grp - 1)
        p1_acols(7)
        if debug and "aT" in debug:
            for g in range(4):
                P.add("sp", lambda e, g=g: e.dma_start(out=yT_d.ap()[g * 128:(g + 1) * 128, :], in_=aT[:, g, :]),
                      r=[("aT", g)], w=[("yT_d", g)], dma=True)
        if upto <= 1:
            break
        wq_pref = load_w(512)
        barrier()
        psn[0] = 8
        A2 = Arena(SH)
        Qss = [A2.bf(64 * 128) for _ in range(2)]
        Ass = [A2.bf(128 * 64) for _ in range(2)]
        dftv = dft.rearrange("p (a b) -> p a b", b=64)
        evi = [0]

        def evac(out, in_, r, w):
            evi[0] += 1
            if evi[0] % 2 == 0:
                P.add("act", lambda e: e.activation(out=out, in_=in_, func=AF.Copy), r=r, w=w)
            else:
                P.add("dve", lambda e: e.tensor_copy(out=out, in_=in_), r=r, w=w)

        def fft_views(g):
            Qs, As = Qss[g % 2], Ass[g % 2]
            return (Qs, As, Qs.rearrange("p (t m) -> p t m", m=128), As.rearrange("p (m k) -> p m k", k=64),
                    ("Qs", g % 2), ("As", g % 2))

        def fft_s1(g):
            Qs, As, Qv, Av, kQ, kA = fft_views(g)
            for tb in range(16):
                bank = zbank()

                def s1(e, tb=tb, bank=bank):
                    for j in range(4):
                        t2 = tb * 4 + j
                        lh = aT[:, g, t2:T:64]
                        e.matmul(PS[bank][0:64, j * 128:(j + 1) * 128], lhsT=lh, rhs=cs128[:, 0:128],
                                 start=True, stop=True)
                        ins = e.matmul(PS[bank][64:128, j * 128:(j + 1) * 128], lhsT=lh, rhs=cs128[:, 128:256],
                                       start=True, stop=True)
                    return ins
                P.add("pe", s1, r=[("aT", g), "const"], w=[("ps", bank)])
                evac(Qs[:, tb * 512:(tb + 1) * 512], PS[bank], [("ps", bank)], [kQ])

        def fft_s2(g):
            Qs, As, Qv, Av, kQ, kA = fft_views(g)
            for mb in range(16):
                bank = zbank()

                def s2(e, mb=mb, bank=bank):
                    for j in range(8):
                        m = mb * 8 + j
                        lh = Qv[:, :, m]
                        e.matmul(PS[bank][0:64, j * 64:(j + 1) * 64], lhsT=lh, rhs=f1[:, 0:64],
                                 start=True, stop=True)
                        ins = e.matmul(PS[bank][64:128, j * 64:(j + 1) * 64], lhsT=lh, rhs=f1[:, 64:128],
                                       start=True, stop=True)
                    return ins
                P.add("pe", s2, r=[kQ, "const"], w=[("ps", bank)])
                evac(As[:, mb * 512:(mb + 1) * 512], PS[bank], [("ps", bank)], [kA])

        def fft_s3(g):
            Qs, As, Qv, Av, kQ, kA = fft_views(g)
            for kb in range(8):
                bank = zbank()

                def s3(e, kb=kb, bank=bank):
                    for j in range(8):
                        k1 = kb * 8 + j
                        ins = e.matmul(PS[bank][:, j * 64:(j + 1) * 64], lhsT=Av[:, :, k1], rhs=dftv[:, k1, :],
                                       start=True, stop=True)
                    return ins
                P.add("pe", s3, r=[kA, "const"], w=[("ps", bank)])
                evac(aT[:, g, kb * 512:(kb + 1) * 512], PS[bank], [("ps", bank)], [("aT", g)])

        fft_s1(0)
        fft_s2(0)
        fft_s1(1)
        fft_s3(0)
        fft_s2(1)
        fft_s1(2)
        fft_s3(1)
        fft_s2(2)
        fft_s1(3)
        fft_s3(2)
        fft_s2(3)
        fft_s3(3)
        if debug and "yT" in debug:
            for g in range(4):
                P.add("sp", lambda e, g=g: e.dma_start(out=yT_d.ap()[g * 128:(g + 1) * 128, :], in_=aT[:, g, :]),
                      r=[("aT", g)], w=[("yT_d", g)], dma=True)
        if upto <= 2:
            break

        barrier()
        A3 = Arena(SH)
        stage = [A3.bf(T) for _ in range(2)]
        tmpa = A3.bf(T)
        thb_ = [A3.bf(512) for _ in range(2)]
        vst = [A3.bf(512) for _ in range(4)]
        xcs = [A3.f32(512) for _ in range(4)]
        sqvs = [A3.f32(512) for _ in range(4)]
        means = [A3.f32(1) for _ in range(4)]
        vars_ = [A3.f32(1) for _ in range(4)]
        rs3s = [A3.f32(1) for _ in range(4)]
        lng = A3.f32(512)
        lnb = A3.f32(512)
        P.add("sp", lambda e, l=l: e.dma_start(out=lng, in_=sgu_lng_in.ap()[l]), w=["lng"], dma=True)
        P.add("sp", lambda e, l=l: e.dma_start(out=lnb, in_=sgu_lnb_in.ap()[l]), w=["lnb"], dma=True)
        stc = [0]
        thc = [0]

        def fm_chunk(wb, kind, dst, row0, wb2=None, only_g=None):
            for g in (range(4) if only_g is None else [only_g]):
                sb_ = stc[0] % 2
                stc[0] += 1
                if kind == "glu":
                    for blk in range(8):
                        bank = zgroup(wb, g, blk)
                        evac(tmpa[:, blk * 512:(blk + 1) * 512], PS[bank], [("ps", bank)], ["tmpa"])
                for blk in range(8):
                    sl = stage[sb_][:, blk * 512:(blk + 1) * 512]
                    bank = zgroup(wb2 if kind == "glu" else wb, g, blk)
                    if kind == "copy":
                        evac(sl, PS[bank], [("ps", bank)], [("stage", sb_)])
                    elif kind == "tanh":
                        P.add("act", lambda e, sl=sl, bank=bank: e.activation(out=sl, in_=PS[bank], func=AF.Sigmoid),
                              r=[("ps", bank)], w=[("stage", sb_)])
                    else:
                        tb = thc[0] % 2
                        thc[0] += 1
                        P.add("act", lambda e, tb=tb, bank=bank: e.activation(out=thb_[tb], in_=PS[bank], func=AF.Sigmoid),
                              r=[("ps", bank)], w=[("th", tb)])
                        if kind == "glu":
                            P.add("dve", lambda e, sl=sl, tb=tb, blk=blk: e.tensor_tensor(
                                out=sl, in0=thb_[tb], in1=tmpa[:, blk * 512:(blk + 1) * 512], op=ALU.mult),
                                r=[("th", tb), "tmpa"], w=[("stage", sb_)])
                        else:
                            P.add("dve", lambda e, sl=sl, tb=tb, bank=bank: e.tensor_tensor(
                                out=sl, in0=PS[bank], in1=thb_[tb], op=ALU.mult),
                                r=[("th", tb), ("ps", bank)], w=[("stage", sb_)])
                            if kind == "gateA":
                                yv = aT[:, g, :].rearrange("p (k1 k2) -> p k2 k1", k2=64)[:, blk * 8:(blk + 1) * 8, :]
                                slv = sl.rearrange("p (r c) -> p r c", c=64)
                                P.add("dve", lambda e, slv=slv, yv=yv: e.tensor_tensor(out=slv, in0=slv, in1=yv, op=ALU.mult),
                                      r=[("aT", g)], w=[("stage", sb_)])
                if dst is ys_d or dst is thm_d:
                    jj_ = (row0 + g * 128) // 128
                    P.add("sp", lambda e, sb_=sb_, jj_=jj_: e.dma_start(
                        out=dst.ap()[:, :, jj_, :].rearrange("b p t -> p b t"),
                        in_=stage[sb_].rearrange("p (b t) -> p b t", b=8)),
                        r=[("stage", sb_)], w=[(dst.name, row0 + g * 128)], dma=True)
                else:
                    P.add("sp", lambda e, sb_=sb_, g=g: e.dma_start(out=dst.ap()[row0 + g * 128:row0 + (g + 1) * 128, :], in_=stage[sb_]),
                          r=[("stage", sb_)], w=[(dst.name, row0 + g * 128)], dma=True)

        def tm_chunk(wb, dst, ln, only_grp=None):
            for i4_ in (range(8) if only_grp is None else [only_grp]):
                tiles = list(range(i4_ * 4, i4_ * 4 + 4))
                banks = {}
                for i in tiles:
                    bank = zbank()
                    banks[i] = bank

                    def mm(e, i=i, bank=bank):
                        for k in range(8):
                            ins = e.matmul(PS[bank], lhsT=hT[:, k, i * 128:(i + 1) * 128], rhs=wbuf[wb][:, k, :],
                                           start=(k == 0), stop=(k == 7))
                        return ins
                    P.add("pe", mm, r=[("wbuf", wb), ("hT", i)], w=[("ps", bank)])
                if not ln:
                    for i in tiles:
                        evac(vst[i % 4], PS[banks[i]], [("ps", banks[i])], [("vst", i % 4)])
                else:
                    for i in tiles:
                        v = i % 4
                        P.add("act", lambda e, bank=banks[i], sqv=sqvs[v], mean=means[v]: e.activation(
                            out=sqv, in_=PS[bank], func=AF.Identity, scale=1.0 / W, accum_out=mean),
                              r=[("ps", banks[i])], w=[("sqv", v), ("mean", v)])
                    for i in tiles:
                        v = i % 4
                        P.add("dve", lambda e, bank=banks[i], xc=xcs[v], mean=means[v]: e.tensor_scalar(
                            out=xc, in0=PS[bank], scalar1=mean, scalar2=None, op0=ALU.subtract),
                              r=[("ps", banks[i]), ("mean", v)], w=[("xc", v)])
                    for i in tiles:
                        v = i % 4
                        P.add("act", lambda e, xc=xcs[v], sqv=sqvs[v], var=vars_[v]: e.activation(
                            out=sqv, in_=xc, func=AF.Square, scale=float(W ** -0.5), accum_out=var),
                              r=[("xc", v)], w=[("sqv", v), ("var", v)])
                    for i in tiles:
                        v = i % 4
                        P.add("pool", lambda e, var=vars_[v]: e.tensor_scalar(out=var, in0=var, scalar1=EPS, scalar2=None, op0=ALU.add),
                              r=[("var", v)], w=[("var", v)])
                    for i in tiles:
                        v = i % 4
                        P.add("pool", lambda e, var=vars_[v], rs3=rs3s[v]: e.tensor_tensor(out=rs3, in0=var, in1=neghalf, op=ALU.pow),
                              r=[("var", v), "const"], w=[("rs3", v)])
                    for i in tiles:
                        v = i % 4
                        P.add("dve", lambda e, xc=xcs[v], rs3=rs3s[v]: e.scalar_tensor_tensor(
                            out=xc, in0=xc, scalar=rs3, in1=lng, op0=ALU.mult, op1=ALU.mult),
                              r=[("xc", v), ("rs3", v), "lng"], w=[("xc", v)])
                    for i in tiles:
                        v = i % 4
                        P.add("dve", lambda e, v=v, xc=xcs[v]: e.tensor_tensor(out=vst[v], in0=xc, in1=lnb, op=ALU.add),
                              r=[("xc", v), "lnb"], w=[("vst", v)])
                for i in tiles:
                    P.add("sp", lambda e, i=i: e.dma_start(out=dst.ap()[i * 128:(i + 1) * 128, :], in_=vst[i % 4]),
                          r=[("vst", i % 4)], w=[(dst.name, i)], dma=True)

        fm_chunk(wq_pref, "copy", qT_d, 0)
        fm_chunk(load_w(1024), "copy", kT_d, 0)
        tm_chunk(load_w(1536), vtok_d, False)
        wa_ = load_w(2048)
        wb_ = load_w(2560)
        fm_chunk(wa_, "glu", hc_d, 0, wb2=wb_)
        wu_ = load_w(3072)
        wv_ = load_w(3584)
        for kind_, idx_ in (("v", 0), ("u", 0), ("v", 1), ("v", 2), ("u", 1), ("v", 3), ("v", 4), ("u", 2),
                            ("v", 5), ("v", 6), ("u", 3), ("v", 7)):
            if kind_ == "u":
                fm_chunk(wu_, "copy", uT_d, 0, only_g=idx_)
            else:
                tm_chunk(wv_, vn_d, True, only_grp=idx_)
        fm_chunk(load_w(4096), "gateA", ys_d, 0)
        for n in range(3):
            fm_chunk(load_w(4608 + 512 * n), "gate", gu_d, 512 * n)
        for c in range(8):
            fm_chunk(load_w(6144 + 512 * c), "tanh", thm_d, 512 * c)
        if upto <= 3:
            break

        barrier()
        psn[0] = 4
        A4 = Arena(BASE)
        qhs = [A4.bf(T) for _ in range(2)]
        khs = [A4.bf(T) for _ in range(2)]
        guhs = [A4.bf(T) for _ in range(2)]
        ysts = [A4.bf(T) for _ in range(2)]
        Kbds = [A4.bf(64 * 128).rearrange("p (r c) -> p r c", c=128) for _ in range(2)]
        Vbds = [A4.bf(64 * 128).rearrange("p (r c) -> p r c", c=128) for _ in range(2)]
        tabs = [A4.f32(960) for _ in range(2)]
        tabb = [A4.bf(960) for _ in range(2)]
        Pt = [A4.bf(512) for _ in range(6)]
        recs = [A4.f32(512) for _ in range(2)]
        def zero_offdiag(s_):
            P.add("dve", lambda e: e.memset(Kbds[s_][0:64, :, 64:128], 0.0), w=[("KbdZ", s_, 0)])
            P.add("pool", lambda e: e.memset(Kbds[s_][64:128, :, 0:64], 0.0), w=[("KbdZ", s_, 1)])
            P.add("dve", lambda e: e.memset(Vbds[s_][0:64, :, 64:128], 0.0), w=[("VbdZ", s_, 0)])
            P.add("pool", lambda e: e.memset(Vbds[s_][64:128, :, 0:64], 0.0), w=[("VbdZ", s_, 1)])

        rs_ = [min(max(r - 4, 0), 56) for r in range(64)]
        wsf = A4.f32(512)
        wsT = A4.bf(512).rearrange("p (g q) -> p g q", g=4)
        sbb = A4.f32(2048).rearrange("p (g r) -> p g r", g=4)
        vng = A4.bf(32 * 128).rearrange("p (i c) -> p i c", i=32)
        ug6 = A4.bf(T)
        gug6 = A4.bf(T)
        st6 = A4.bf(T)
        mixs = [A4.bf(512) for _ in range(4)]
        assert A4.off <= NBF, A4.off
        P.add("sp", lambda e, l=l: e.dma_start(out=wsf, in_=sgu_wT_in.ap()[l].rearrange("q g p -> q (g p)")), w=["wsf"], dma=True)
        P.add("sp", lambda e, l=l: e.dma_start(out=sbb, in_=sgu_b_in.ap()[l]), w=["sbb"], dma=True)
        P.add("dve", lambda e: e.tensor_copy(out=wsT.rearrange("p g q -> p (g q)"), in_=wsf), r=["wsf"], w=["wsT"])

        def sgu_loads(g):
            P.add("sp", lambda e: e.dma_start(out=ug6, in_=uT_d.ap()[g * 128:(g + 1) * 128, :]),
                  r=[("uT", g * 128)], w=["ug6"], dma=True)
            P.add("sp", lambda e: e.dma_start(out=gug6, in_=gu_d.ap()[2 * W + g * 128:2 * W + (g + 1) * 128, :]),
                  r=[("gu", 2 * W + g * 128)], w=["gug6"], dma=True)
            for iq in range(4):
                P.add("sp", lambda e, iq=iq: e.dma_start(
                    out=vng[:, iq * 8:(iq + 1) * 8, :],
                    in_=vn_d.ap()[iq * 1024:(iq + 1) * 1024, g * 128:(g + 1) * 128].rearrange("(i p) c -> p i c", p=128)),
                    r=[("vn", i) for i in range(8 * iq, 8 * iq + 8)], w=[("vng", iq)], dma=True)
            for hq in range(2):
                hs = slice(hq * 2048, (hq + 1) * 2048)
                P.add("pool", lambda e, hs=hs: e.tensor_tensor(out=gug6[:, hs], in0=gug6[:, hs], in1=ug6[:, hs], op=ALU.mult),
                      r=["gug6", "ug6"], w=["gug6"])

        def sgu_bank(g, blk):
            bank = zbank()
            q6 = blk % 4
            bs = slice(blk * 512, (blk + 1) * 512)

            def smm(e):
                for c in range(4):
                    i = blk * 4 + c
                    ins = e.matmul(PS[bank][:, c * 128:(c + 1) * 128], lhsT=vng[:, i, :], rhs=wsT[:, g, :],
                                   start=True, stop=True)
                return ins
            P.add("pe", smm, r=[("vng", blk // 2), "wsT"], w=[("ps", bank)])
            P.add("dve", lambda e: e.tensor_tensor(out=mixs[q6], in0=PS[bank], in1=sbb[:, g, :], op=ALU.add),
                  r=[("ps", bank), "sbb"], w=[("mix", q6)])
            P.add("dve", lambda e: e.tensor_tensor(out=st6[:, bs], in0=mixs[q6], in1=gug6[:, bs], op=ALU.mult),
                  r=[("mix", q6), "gug6"], w=["st6"])
            if blk == 7:
                P.add("sp", lambda e: e.dma_start(out=ys_d.ap()[:, :, 12 + g, :].rearrange("b p t -> p b t"),
                                                  in_=st6.rearrange("p (b t) -> p b t", b=8)),
                      r=["st6"], w=[("ys", 3 * W + g * 128)], dma=True)

        def load_hp(hp, l=l, part="all"):
            s_ = hp % 2
            if part == "fill":
                P.add("dve", lambda e: e.tensor_scalar(out=tabb[s_], in0=tabs[s_], scalar1=8.0, scalar2=None, op0=ALU.mult),
                      r=[("tabf", s_)], w=[("tab", s_)])
                for half in range(2):
                    lo, hi = half * 64, half * 64 + 64
                    kin = khs[s_][lo:hi, :].rearrange("p (r c) -> p r c", c=64)
                    P.add("dve", lambda e, lo=lo, hi=hi, kin=kin: e.tensor_copy(out=Kbds[s_][lo:hi, :, lo:hi], in_=kin),
                          r=[("kh", s_)], w=[("Kbd", s_, half)])
                return
            P.add("sp", lambda e: e.dma_start(out=khs[s_], in_=kT_d.ap()[hp * 128:(hp + 1) * 128, :]),
                  r=[("kT", hp * 128)], w=[("kh", s_)], dma=True)
            P.add("sp", lambda e: e.dma_start(out=qhs[s_], in_=qT_d.ap()[hp * 128:(hp + 1) * 128, :]),
                  r=[("qT", hp * 128)], w=[("qh", s_)], dma=True)
            P.add("sp", lambda e: e.dma_start(out=tabs[s_], in_=tab_in.ap()[l, hp]), w=[("tabf", s_)], dma=True)
            if part == "all":
                P.add("dve", lambda e: e.tensor_scalar(out=tabb[s_], in0=tabs[s_], scalar1=8.0, scalar2=None, op0=ALU.mult),
                      r=[("tabf", s_)], w=[("tab", s_)])
            for half in range(2):
                lo, hi = half * 64, half * 64 + 64
                src = vtok_d.ap()[:, hp * 128 + lo:hp * 128 + hi].rearrange("(r c) d -> c r d", c=64)
                vq = "pool" if (hp > 0 or half == 0) else "sp"
                for rq in range(4):
                    P.add(vq, lambda e, lo=lo, hi=hi, src=src, rq=rq: e.dma_start(
                        out=Vbds[s_][lo:hi, rq * 16:(rq + 1) * 16, lo:hi], in_=src[:, rq * 16:(rq + 1) * 16, :]),
                        r=[("vtok", i) for i in range(8 * rq, 8 * rq + 8)], w=[("Vbd", s_, half)], dma=True)
                kin = khs[s_][lo:hi, :].rearrange("p (r c) -> p r c", c=64)
                if hp == 0 and half == 0:
                    P.add("dve", lambda e, lo=lo, hi=hi, kin=kin: e.tensor_copy(out=Kbds[s_][lo:hi, :, lo:hi], in_=kin),
                          r=[("kh", s_)], w=[("Kbd", s_, half)])
                elif hp == 0:
                    P.add("dve", lambda e, lo=lo, hi=hi, kin=kin: e.tensor_copy(out=Kbds[s_][lo:hi, :, lo:hi], in_=kin),
                          r=[("kh", s_)], w=[("Kbd", s_, half)])
                elif part == "all":
                    P.add("dve", lambda e, lo=lo, hi=hi, kin=kin: e.tensor_copy(out=Kbds[s_][lo:hi, :, lo:hi], in_=kin),
                          r=[("kh", s_)], w=[("Kbd", s_, half)])
            P.add("sp", lambda e: e.dma_start(out=guhs[s_], in_=gu_d.ap()[hp * 128:(hp + 1) * 128, :]),
                  r=[("gu", hp * 128)], w=[("guh", s_)], dma=True)

        load_hp(0)
        zero_offdiag(0)
        zero_offdiag(1)
        sgu_loads(0)
        for hp in range(4):
            s_ = hp % 2
            qh, guh, yst, Kbd, Vbd, tab = qhs[s_], guhs[s_], ysts[s_], Kbds[s_], Vbds[s_], tabb[s_]
            kKbd = [("Kbd", s_, 0), ("Kbd", s_, 1), ("KbdZ", s_, 0), ("KbdZ", s_, 1)]
            kVbd = [("Vbd", s_, 0), ("Vbd", s_, 1), ("VbdZ", s_, 0), ("VbdZ", s_, 1)]
            kqh, ktab, kguh, kyst = ("qh", s_), ("tab", s_), ("guh", s_), ("yst", s_)
            if hp + 1 < 4:
                load_hp(hp + 1, part="dma")
            work = []
            for j in range(8):
                r0 = 8 * j
                pieces = []
                for kr in range(64):
                    rr = [r for r in range(r0, r0 + 8) if rs_[r] <= kr <= rs_[r] + 7]
                    if rr:
                        pieces.append((kr, rr[0], rr[-1]))
                pieces.sort(key=lambda p_: -(p_[2] - p_[1]))
                assert pieces[0][1] == r0 and pieces[0][2] == r0 + 7
                for pi, (kr, ra, rb) in enumerate(pieces):
                    work.append((j, r0, kr, ra, rb, pi == 0, pi == len(pieces) - 1))
            LOOK = 3
            NB = 6
            for t in range(len(work) + LOOK):
                if t == 34 and hp + 1 < 4:
                    load_hp(hp + 1, part="fill")
                if t >= 16 and t % 12 == 4 and (t - 16) // 12 < 8:
                    sgu_bank(hp, (t - 16) // 12)
                    if (t - 16) // 12 == 7 and hp + 1 < 4:
                        sgu_loads(hp + 1)
                if t < len(work):
                    j, r0, kr, ra, rb, first, lastp = work[t]
                    n64 = (rb - ra + 1) * 64
                    i0 = 7 - kr + ra
                    assert 0 <= i0 and i0 + (rb - ra) <= 14
                    sbk = zbank()
                    b2 = t % NB
                    def smm4(e, sbk=sbk, kr=kr, ra=ra, rb=rb, n64=n64, Kbd=Kbd, qh=qh, tab=tab, i0=i0):
                        e.matmul(PS[sbk][:, 0:n64], lhsT=Kbd[:, kr, :], rhs=qh[:, ra * 64:(rb + 1) * 64], start=True, stop=False)
                        return e.matmul(PS[sbk][:, 0:n64], lhsT=ident, rhs=tab[:, i0 * 64:i0 * 64 + n64], start=False, stop=True)
                    P.add("pe", smm4, r=kKbd + [kqh, ktab, "const"], w=[("ps", sbk)])
                    P.add("act", lambda e, sbk=sbk, b2=b2, n64=n64: e.activation(out=Pt[b2][:, 0:n64], in_=PS[sbk][:, 0:n64],
                                                                                 func=AF.Exp, scale=0.125),
                          r=[("ps", sbk)], w=[("Pt", b2)])
                if t >= LOOK:
                    u = t - LOOK
                    j, r0, kr, ra, rb, first, lastp = work[u]
                    n64 = (rb - ra + 1) * 64
                    b2 = u % NB
                    yb, db = 4 + (j % 2), 6 + (j % 2)
                    c0, c1 = (ra - r0) * 64, (rb - r0 + 1) * 64

                    def pv(e, yb=yb, db=db, kr=kr, b2=b2, n64=n64, c0=c0, c1=c1, first=first, lastp=lastp, Vbd=Vbd):
                        e.matmul(PS[yb][:, c0:c1], lhsT=Vbd[:, kr, :], rhs=Pt[b2][:, 0:n64], start=first, stop=lastp,
                                 skip_group_check=True)
                        return e.matmul(PS[db][:, c0:c1], lhsT=onesbd, rhs=Pt[b2][:, 0:n64], start=first, stop=lastp,
                                        skip_group_check=True)
                    P.add("pe", pv, r=kVbd + [("Pt", b2), "const"], w=[("ps", yb), ("ps", db)])
                    if lastp:
                        rc = recs[j % 2]
                        P.add("act", lambda e, db=db, rc=rc: e.activation(out=rc, in_=PS[db], func=AF.Ln), r=[("ps", db)], w=[("rec", j % 2)])
                        P.add("act", lambda e, rc=rc: e.activation(out=rc, in_=rc, func=AF.Exp, scale=-1.0),
                              r=[("rec", j % 2)], w=[("rec", j % 2)])
                        P.add("dve", lambda e, yb=yb, rc=rc: e.tensor_tensor(out=rc, in0=PS[yb], in1=rc, op=ALU.mult),
                              r=[("ps", yb), ("rec", j % 2)], w=[("rec", j % 2)])
                        P.add("dve", lambda e, r0=r0, rc=rc, yst=yst, guh=guh: e.tensor_tensor(
                            out=yst[:, r0 * 64:(r0 + 8) * 64], in0=rc, in1=guh[:, r0 * 64:(r0 + 8) * 64], op=ALU.mult),
                              r=[("rec", j % 2), kguh], w=[kyst])
            P.add("sp", lambda e, hp=hp, yst=yst: e.dma_start(out=ys_d.ap()[:, :, 4 + hp, :].rearrange("b p t -> p b t"),
                                                              in_=yst.rearrange("p (b t) -> p b t", b=8)),
                  r=[kyst], w=[("ys", W + hp * 128)], dma=True)
        if upto <= 4:
            break

        barrier()
        psn[0] = 8
        A5 = Arena(BASE)
        hcps = [A5.bf(T + 32) for _ in range(4)]
        diags = [A5.bf(31 * 128).rearrange("p (j c) -> p j c", c=128) for _ in range(4)]
        cws = [A5.f32(31) for _ in range(4)]
        cv = [A5.f32(3) for _ in range(4)]
        hg = [A5.f32(1) for _ in range(4)]
        hbv = [A5.f32(1) for _ in range(4)]
        cxbs = [A5.bf(4 * 512).rearrange("p (g t) -> p g t", g=4) for _ in range(2)]
        sqbs = [A5.bf(4 * 512).rearrange("p (g t) -> p g t", g=4) for _ in range(2)]
        gucs = [A5.bf(4 * 512).rearrange("p (g t) -> p g t", g=4) for _ in range(2)]
        st5s = [A5.bf(4 * 512).rearrange("p (g t) -> p g t", g=4) for _ in range(2)]
        mean5 = [A5.f32(512) for _ in range(2)]
        m2s = [A5.f32(512) for _ in range(2)]
        var5 = [A5.f32(512) for _ in range(2)]
        rstd5 = [A5.f32(512) for _ in range(2)]
        t5s = [A5.f32(512) for _ in range(4)]
        y5s = [A5.bf(512) for _ in range(4)]
        th5s = [A5.bf(512) for _ in range(4)]
        assert A5.off <= TOPW, A5.off
        for g in range(4):
            P.add("dve", lambda e, g=g: e.memset(hcps[g][:, 0:16], 0.0), w=[("hcpZ", g)])
            P.add("dve", lambda e, g=g: e.memset(hcps[g][:, 16 + T:32 + T], 0.0), w=[("hcpZ", g)])
        for g in range(4):
            P.add("sp", lambda e, g=g: e.dma_start(out=hcps[g][:, 16:16 + T], in_=hc_d.ap()[g * 128:(g + 1) * 128, :]),
                  r=[("hc", g * 128)], w=[("hcp", g)], dma=True)
            P.add("sp", lambda e, g=g, l=l: e.dma_start(out=cws[g], in_=convw_in.ap()[l, g * 128:(g + 1) * 128, :]), w=[("cw", g)], dma=True)
            P.add("sp", lambda e, g=g, l=l: e.dma_start(out=cv[g], in_=cvec_in.ap()[l, g * 128:(g + 1) * 128, :]), w=[("cv", g)], dma=True)
        for g in range(4):
            P.add("dve", lambda e, g=g: e.tensor_scalar(out=hg[g], in0=cv[g][:, 1:2], scalar1=0.5, scalar2=None, op0=ALU.mult),
                  r=[("cv", g)], w=[("hg", g)])
            P.add("dve", lambda e, g=g: e.tensor_scalar(out=hbv[g], in0=cv[g][:, 2:3], scalar1=0.5, scalar2=None, op0=ALU.mult),
                  r=[("cv", g)], w=[("hbv", g)])
            for jj in range(31):
                if jj % 2 == 0:
                    P.add("dve", lambda e, jj=jj, g=g: e.tensor_scalar(out=diags[g][:, jj, :], in0=ident, scalar1=cws[g][:, jj:jj + 1],
                                                                       scalar2=None, op0=ALU.mult),
                          r=[("cw", g), "const"], w=[("diag", g, jj)])
                else:
                    P.add("act", lambda e, jj=jj, g=g: e.activation(out=diags[g][:, jj, :], in_=ident, func=AF.Copy,
                                                                    scale=cws[g][:, jj:jj + 1]),
                          r=[("cw", g), "const"], w=[("diag", g, jj)])
        P.add("pool", lambda e, l=l: e.dma_start(out=wbr, in_=wbr_in.ap()[l].rearrange("(j p) d -> p j d", p=128)), w=["wbr"], dma=True)
        P.add("pool", lambda e, l=l: e.dma_start(out=wo, in_=wout_in.ap()[l].rearrange("(j p) d -> p j d", p=128)), w=["wo"], dma=True)

        def conv_epi(b):
            q = b % 2
            bs = slice(b * 512, (b + 1) * 512)
            cx, sqb, guc, st5 = cxbs[q], sqbs[q], gucs[q], st5s[q]
            kcx = [("cxb", q, g) for g in range(4)]
            P.add("pool", lambda e: e.tensor_tensor(out=sqb, in0=cx, in1=cx, op=ALU.mult), r=kcx, w=[("sqb", q)])
            mb, vb_ = zbank(), zbank()

            def stat(e):
                for g in range(4):
                    e.matmul(PS[mb], lhsT=ones512, rhs=cx[:, g, :], start=(g == 0), stop=(g == 3))
                for g in range(4):
                    ins = e.matmul(PS[vb_], lhsT=ones512, rhs=sqb[:, g, :], start=(g == 0), stop=(g == 3))
                return ins
            P.add("pe", stat, r=kcx + [("sqb", q), "const"], w=[("ps", mb), ("ps", vb_)])
            P.add("dve", lambda e: e.tensor_copy(out=mean5[q], in_=PS[mb]), r=[("ps", mb)], w=[("mean5", q)])
            P.add("dve", lambda e: e.tensor_tensor(out=m2s[q], in0=mean5[q], in1=mean5[q], op=ALU.mult),
                  r=[("mean5", q)], w=[("m2", q)])
            P.add("dve", lambda e: e.tensor_tensor(out=var5[q], in0=PS[vb_], in1=m2s[q], op=ALU.subtract),
                  r=[("ps", vb_), ("m2", q)], w=[("var5", q)])
            P.add("act", lambda e: e.activation(out=var5[q], in_=var5[q], func=AF.Ln, bias=eps_t), r=[("var5", q), "const"], w=[("var5", q)])
            P.add("act", lambda e: e.activation(out=rstd5[q], in_=var5[q], func=AF.Exp, scale=-0.5), r=[("var5", q)], w=[("rstd5", q)])
            for g in range(4):
                P.add("pool", lambda e, g=g: e.tensor_tensor(out=t5s[g], in0=cx[:, g, :], in1=mean5[q], op=ALU.subtract),
                      r=[("cxb", q, g), ("mean5", q)], w=[("t5", g)])
            for g in range(4):
                P.add("dve", lambda e, g=g: e.tensor_tensor(out=t5s[g], in0=t5s[g], in1=rstd5[q], op=ALU.mult),
                      r=[("t5", g), ("rstd5", q)], w=[("t5", g)])
            for g in range(4):
                P.add("act", lambda e, g=g: e.activation(out=th5s[g], in_=t5s[g], func=AF.Tanh, scale=hg[g], bias=hbv[g]),
                      r=[("t5", g), ("hg", g), ("hbv", g)], w=[("th5", g)])
                P.add("act", lambda e, g=g: e.activation(out=y5s[g], in_=t5s[g], func=AF.Identity, scale=hg[g], bias=hbv[g]),
                      r=[("t5", g), ("hg", g), ("hbv", g)], w=[("y5", g)])
            for g in range(4):
                P.add("dve", lambda e, g=g: e.scalar_tensor_tensor(out=y5s[g], in0=th5s[g], scalar=1.0, in1=y5s[g],
                                                                   op0=ALU.add, op1=ALU.mult),
                      r=[("th5", g), ("y5", g)], w=[("y5", g)])
            for g in range(4):
                P.add("dve", lambda e, g=g: e.tensor_tensor(out=st5[:, g, :], in0=y5s[g], in1=guc[:, g, :], op=ALU.mult),
                      r=[("y5", g), ("guc", q)], w=[("st5", q)])
            P.add("sp", lambda e: e.dma_start(out=ys_d.ap()[b, :, 8:12, :], in_=st5),
                  r=[("st5", q)], w=[("ys", 2 * W + g_ * 128) for g_ in range(4)], dma=True)

        for blk in range(9):
            if blk < 8:
                q = blk % 2
                bs = slice(blk * 512, (blk + 1) * 512)
                P.add("sp", lambda e, q=q, bs=bs: e.dma_start(out=gucs[q], in_=gu_d.ap()[W:2 * W, bs].rearrange("(g c) t -> c g t", c=128)),
                      r=[("gu", W + g_ * 128) for g_ in range(4)], w=[("guc", q)], dma=True)
                for g in range(4):
                    bank = zbank()

                    def cmm(e, bank=bank, blk=blk, g=g):
                        for jj in range(31):
                            o = 16 + blk * 512 + jj - 15
                            ins = e.matmul(PS[bank], lhsT=diags[g][:, jj, :], rhs=hcps[g][:, o:o + 512], start=(jj == 0), stop=(jj == 30))
                        return ins
                    P.add("pe", cmm, r=[("diag", g, jj) for jj in range(31)] + [("hcp", g), ("hcpZ", g)], w=[("ps", bank)])
                    P.add("act", lambda e, g=g, q=q, bank=bank: e.activation(
                        out=cxbs[q][:, g, :], in_=PS[bank], func=AF.Identity, scale=1.0, bias=cv[g][:, 0:1]),
                        r=[("ps", bank), ("cv", g)], w=[("cxb", q, g)])
            if blk >= 1:
                conv_epi(blk - 1)
        if upto <= 5:
            break

        barrier()
        A7 = Arena(BASE)
        ysbs = [A7.bf(16 * 512).rearrange("p (j t) -> p j t", j=16) for _ in range(2)]
        thms = [thm0, A7.bf(32 * 512).rearrange("p (j t) -> p j t", j=32)]
        mTs = [A7.bf(8 * 512).rearrange("p (j t) -> p j t", j=8) for _ in range(2)]
        tns = [[A7.bf(512) for _ in range(4)] for _ in range(2)]
        pbfs = [A7.bf(512) for _ in range(6)]
        xt7 = [A7.f32(D) for _ in range(3)]
        ot7 = xt7
        sq7s = [A7.bf(D) for _ in range(2)]
        ms7s = [A7.f32(1) for _ in range(3)]
        rs7s = [A7.f32(1) for _ in range(3)]
        fg_bc = A7.f32(D)
        assert A7.off <= TOP0, (A7.off, TOP0)
        P.add("sp", lambda e: e.dma_start(out=fg_bc, in_=fg_in.ap()), w=["fg_bc"], dma=True)
        ys_all = [("ys", r_) for r_ in range(0, 4 * W, 128)]

        def load7(blk, only_thm=False, skip_thm=False):
            bs = slice(blk * 512, (blk + 1) * 512)
            ysb, thm = ysbs[blk % 2], thms[blk % 2]
            if not only_thm:
                for hf in range(2):
                    P.add("act", lambda e, hf=hf: e.dma_start(
                        out=ysb[:, hf * 8:(hf + 1) * 8, :],
                        in_=ys_d.ap()[blk, :, hf * 8:(hf + 1) * 8, :]),
                        r=ys_all[hf * 8:(hf + 1) * 8], w=[("ysb", blk % 2, hf)], dma=True)
            if not skip_thm:
                for qt in range(4):
                    P.add("sp", lambda e, qt=qt: e.dma_start(
                        out=thm[:, qt * 8:(qt + 1) * 8, :],
                        in_=thm_d.ap()[blk, :, qt * 8:(qt + 1) * 8, :]),
                        r=[("thm", r_) for r_ in range(qt * 1024, (qt + 1) * 1024, 128)], w=[("thm", blk % 2, qt)], dma=True)

        def outproj_a(blk, tt, l=l, xs=xs):
            mT = mTs[blk % 2]
            if True:
                i = blk * 4 + tt
                b = i % 3
                P.add("sp", lambda e, i=i, b=b: e.dma_start(out=xt7[b], in_=xs[i * 128:(i + 1) * 128, :]),
                      r=[("x", l)], w=[("xt7", b)], dma=True)
                for hf in range(2):
                    bank = zbank()

                    def omm(e, bank=bank, tt=tt, hf=hf):
                        for kd in range(8):
                            ins = e.matmul(PS[bank], lhsT=mT[:, kd, tt * 128:(tt + 1) * 128], rhs=wo[:, kd, hf * 512:(hf + 1) * 512],
                                           start=(kd == 0), stop=(kd == 7))
                        return ins
                    P.add("pe", omm, r=[("mT", blk % 2), "wo"], w=[("ps", bank)])
                    P.add("dve", lambda e, bank=bank, b=b, hf=hf: e.scalar_tensor_tensor(
                        out=ot7[b][:, hf * 512:(hf + 1) * 512], in0=PS[bank], scalar=1.0, in1=xt7[b][:, hf * 512:(hf + 1) * 512],
                        op0=ALU.mult, op1=ALU.add), r=[("ps", bank), ("xt7", b)], w=[("xt7", b), ("ot7", b)])

        def outproj_b(blk, tt, l=l, xdst=xdst, last=last):
            if True:
                i = blk * 4 + tt
                b = i % 3
                if last:
                    sq7, ms7, rs7 = sq7s[i % 2], ms7s[b], rs7s[b]
                    P.add("act", lambda e, b=b, sq7=sq7, ms7=ms7: e.activation(out=sq7, in_=ot7[b], func=AF.Square, scale=1.0 / 32.0,
                                                                               accum_out=ms7),
                          r=[("ot7", b)], w=[("sq7", i % 2), ("ms7", b)])
                    P.add("pool", lambda e, ms7=ms7: e.tensor_scalar(out=ms7, in0=ms7, scalar1=EPS, scalar2=None, op0=ALU.add),
                          r=[("ms7", b)], w=[("ms7", b)])
                    P.add("pool", lambda e, ms7=ms7, rs7=rs7: e.tensor_tensor(out=rs7, in0=ms7, in1=neghalf, op=ALU.pow),
                          r=[("ms7", b), "const"], w=[("rs7", b)])
                    P.add("dve", lambda e, b=b, rs7=rs7: e.scalar_tensor_tensor(out=ot7[b], in0=ot7[b], scalar=rs7, in1=fg_bc,
                                                                               op0=ALU.mult, op1=ALU.mult),
                          r=[("ot7", b), ("rs7", b), "fg_bc"], w=[("ot7", b)])
                P.add("pool", lambda e, i=i, b=b: e.dma_start(out=xdst.ap()[i * 128:(i + 1) * 128, :], in_=ot7[b]),
                      r=[("ot7", b), ("xt7", b)], w=[("x", l + 1), ("xrow", l, i)], dma=True)

        load7(0)
        for blk in range(8):
            ysb, thm, mT = ysbs[blk % 2], thms[blk % 2], mTs[blk % 2]
            if blk + 1 < 8:
                load7(blk + 1)
            for dg in range(8):
                tn = tns[dg % 2]
                for n in range(4):
                    bank = zbank()

                    def pmm(e, bank=bank, n=n, dg=dg, ysb=ysb):
                        for kc in range(4):
                            ins = e.matmul(PS[bank], lhsT=wbr[:, n * 4 + kc, dg * 128:(dg + 1) * 128], rhs=ysb[:, n * 4 + kc, :],
                                           start=(kc == 0), stop=(kc == 3))
                        return ins
                    P.add("pe", pmm, r=["wbr", ("ysb", blk % 2, n // 2)], w=[("ps", bank)])
                    pi_ = (dg * 4 + n) % 6
                    P.add("act", lambda e, bank=bank, pi_=pi_: e.activation(out=pbfs[pi_], in_=PS[bank], func=AF.Copy),
                          r=[("ps", bank)], w=[("pbf", pi_)])
                    P.add("dve", lambda e, pi_=pi_, n=n, dg=dg, tn=tn, thm=thm: e.tensor_tensor(
                        out=tn[n], in0=thm[:, n * 8 + dg, :], in1=pbfs[pi_], op=ALU.mult),
                        r=[("thm", blk % 2, n), ("pbf", pi_)], w=[("tn", dg % 2, n)])
                P.add("dve", lambda e, tn=tn: e.tensor_tensor(out=tn[2], in0=tn[2], in1=tn[3], op=ALU.add),
                      r=[("tn", dg % 2, 2), ("tn", dg % 2, 3)], w=[("tn", dg % 2, 2)])
                P.add("dve", lambda e, tn=tn: e.tensor_tensor(out=tn[0], in0=tn[0], in1=tn[1], op=ALU.add),
                      r=[("tn", dg % 2, 0), ("tn", dg % 2, 1)], w=[("tn", dg % 2, 0)])
                P.add("dve", lambda e, tn=tn, dg=dg, mT=mT: e.tensor_tensor(out=mT[:, dg, :], in0=tn[0], in1=tn[2], op=ALU.add),
                      r=[("tn", dg % 2, 0), ("tn", dg % 2, 2)], w=[("mT", blk % 2)])
                if dg == 3 and blk > 0:
                    for tt in range(4):
                        outproj_a(blk - 1, tt)
                        outproj_b(blk - 1, tt)
        for tt in range(4):
            outproj_a(7, tt)
            outproj_b(7, tt)

    P.emit(nc, stack)
    stack.close()
    return nc


_CONSTS = None


def make_in_maps(inputs):
    global _CONSTS
    if _CONSTS is None:
        _CONSTS = host_consts()
    c = _CONSTS
    f = lambda a: np.ascontiguousarray(np.asarray(a, dtype=np.float32))
    x = f(inputs["x"])
    bc = lambda v: np.ascontiguousarray(np.broadcast_to(v[:, None, :], (v.shape[0], 128, v.shape[1])))
    shared = {
        "norm_g_bc": bc(f(inputs["norm_g"])),
        "final_g_bc": np.ascontiguousarray(np.broadcast_to(f(inputs["final_g"])[None, :], (128, D))),
        "w_in": f(inputs["w_in"]),
        "w_branch": f(inputs["w_branch"]).reshape(L, 4 * W, D),
        "w_out": f(inputs["w_out"]),
        "attn_tab": attn_tables(f(inputs["nat_rpb"])),
        "conv_wT": np.ascontiguousarray(f(inputs["conv_w"]).transpose(0, 2, 1)),
        "cvec": np.ascontiguousarray(np.stack([f(inputs["conv_b"]), f(inputs["conv_ln_g"]),
                                               f(inputs["conv_ln_b"])], axis=-1)),
        "sgu_ln_g_bc": bc(f(inputs["sgu_ln_g"])),
        "sgu_ln_b_bc": bc(f(inputs["sgu_ln_b"])),
        "sgu_wT": np.ascontiguousarray(f(inputs["sgu_w"]).transpose(0, 3, 1, 2)),
        "sgu_b_bc": np.ascontiguousarray(np.broadcast_to(f(inputs["sgu_b"])[:, None, :, None, :], (L, 128, 4, 4, 128))).reshape(L, 128, 4, 512),
        "c_ident": c["ident"], "c_cs128": c["cs128"], "c_f1": c["f1"], "c_dft": c["dft"],
        "c_onesbd": c["onesbd"], "c_ones512": c["ones512"], "c_neghalf": c["neghalf"],
    }
    maps = []
    for b in range(8):
        m = dict(shared)
        m["x"] = np.ascontiguousarray(x[b])
        maps.append(m)
    return maps


def kernel(**inputs):
    nc = build()
    res = run_bass_kernel_spmd(nc, make_in_maps(inputs), core_ids=list(range(8)))
    return np.stack([np.asarray(r["out"], dtype=np.float32) for r in res.results], axis=0)
```
